# Optimizing a Trainium2 kernel written in Bass

```python
import math
import jax, jax.numpy as jnp
from jax import lax
import numpy as np

D_MODEL = 2048
BATCH = 4
SEQ = 2048
DEPTH = 1

CTX_LEN = 256
GRID_W = 64

HEAD_DIM = 128
DA_HEADS = 8
DA_HALF = HEAD_DIM // 2
MLA_HEADS = 8
MLA_NOPE = 128
MLA_ROPE = 64
MLA_V = 128
Q_RANK = 384
KV_RANK = 256
ROPE_DIM = 64
ROPE_BASE = 10000.0
D_FF = 5632
CONV_W = 3
N_MOD = 6
EPS = 1e-6
Q_BLOCK = 128

DA_WIDTH = DA_HEADS * HEAD_DIM
MLA_WIDTH = MLA_HEADS * MLA_V
MIX_WIDTH = DA_WIDTH + MLA_WIDTH
MLA_QK = MLA_NOPE + MLA_ROPE
IN_WIDTH = 3 * DA_WIDTH + Q_RANK + KV_RANK + MLA_ROPE
DA_SCALE = 1.0 / math.sqrt(DA_HALF)
MLA_SCALE = 1.0 / math.sqrt(MLA_QK)

kernel_name = "hybrid_diffattn_mla_convffn_dit_layer"


def _rmsnorm(x, w):
    xf = x.astype(jnp.float32)
    y = xf * lax.rsqrt(jnp.mean(xf * xf, axis=-1, keepdims=True) + EPS)
    return y.astype(x.dtype) * w


def _modulate(h, shift, scale):
    return h * (1.0 + scale) + shift


def _axial_rope_tables(n_tokens):
    rows = n_tokens // GRID_W
    row = jnp.broadcast_to(jnp.arange(rows)[:, None], (rows, GRID_W)).reshape(-1).astype(jnp.float32)
    col = jnp.broadcast_to(jnp.arange(GRID_W)[None, :], (rows, GRID_W)).reshape(-1).astype(jnp.float32)
    nf = ROPE_DIM // 4
    inv = ROPE_BASE ** (-jnp.arange(nf, dtype=jnp.float32) / nf)
    ang_r = row[:, None] * inv
    ang_c = col[:, None] * inv
    return (jnp.cos(ang_r), jnp.sin(ang_r), jnp.cos(ang_c), jnp.sin(ang_c))


def _rot(x, cos, sin):
    cos = cos.astype(x.dtype)
    sin = sin.astype(x.dtype)
    x1, x2 = jnp.split(x, 2, axis=-1)
    return jnp.concatenate([x1 * cos - x2 * sin, x2 * cos + x1 * sin], axis=-1)


def _axial_rope(x, tabs):
    cos_r, sin_r, cos_c, sin_c = tabs
    xr, xc = jnp.split(x, 2, axis=-1)
    return jnp.concatenate([_rot(xr, cos_r, sin_r), _rot(xc, cos_c, sin_c)], axis=-1)


def _mixer_inputs(h, tabs, w_in, q_norm_w, kv_norm_w, w_uq, w_ukv):
    B, T, _ = h.shape
    p = h @ w_in
    o1, o2, o3 = DA_WIDTH, 2 * DA_WIDTH, 3 * DA_WIDTH
    o4 = o3 + Q_RANK
    o5 = o4 + KV_RANK

    def heads(a, n):
        return a.reshape(B, T, n, -1).transpose(0, 2, 1, 3)

    q_da = heads(p[..., :o1], DA_HEADS)
    k_da = heads(p[..., o1:o2], DA_HEADS)
    v_da = heads(p[..., o2:o3], DA_HEADS)
    c_q = _rmsnorm(p[..., o3:o4], q_norm_w)
    c_kv = _rmsnorm(p[..., o4:o5], kv_norm_w)
    k_rope = p[..., o5:][:, None]
    q_mla = heads(c_q @ w_uq, MLA_HEADS)
    kv = heads(c_kv @ w_ukv, MLA_HEADS)
    q_nope, q_rope = q_mla[..., :MLA_NOPE], q_mla[..., MLA_NOPE:]
    k_nope, v_mla = kv[..., :MLA_NOPE], kv[..., MLA_NOPE:]
    if tabs is not None:
        q_da = jnp.concatenate([_axial_rope(q_da[..., :DA_HALF], tabs), _axial_rope(q_da[..., DA_HALF:], tabs)], -1)
        k_da = jnp.concatenate([_axial_rope(k_da[..., :DA_HALF], tabs), _axial_rope(k_da[..., DA_HALF:], tabs)], -1)
        q_rope = _axial_rope(q_rope, tabs)
        k_rope = _axial_rope(k_rope, tabs)
    q_mla = jnp.concatenate([q_nope, q_rope], axis=-1)
    k_mla = jnp.concatenate([k_nope, jnp.broadcast_to(k_rope, (B, MLA_HEADS, T, MLA_ROPE))], axis=-1)
    return q_da, k_da, v_da, q_mla, k_mla, v_mla


def _diff_attend(q, k, v, lam):
    q1, q2 = jnp.split(q, 2, axis=-1)
    k1, k2 = jnp.split(k, 2, axis=-1)
    s1 = jnp.einsum('bhqd,bhkd->bhqk', q1, k1).astype(jnp.float32) * DA_SCALE
    s2 = jnp.einsum('bhqd,bhkd->bhqk', q2, k2).astype(jnp.float32) * DA_SCALE
    w = jax.nn.softmax(s1, axis=-1) - lam * jax.nn.softmax(s2, axis=-1)
    return jnp.einsum('bhqk,bhkd->bhqd', w.astype(v.dtype), v)


def _softmax_attend(q, k, v, scale):
    s = jnp.einsum('bhqd,bhkd->bhqk', q, k).astype(jnp.float32) * scale
    p = jax.nn.softmax(s, axis=-1)
    return jnp.einsum('bhqk,bhkd->bhqd', p.astype(v.dtype), v)


def _sweep(block_fn, q):
    B, H, S, d = q.shape
    nb = S // Q_BLOCK
    qb = q.reshape(B, H, nb, Q_BLOCK, d).transpose(2, 0, 1, 3, 4)
    out = lax.map(block_fn, qb)
    return out.transpose(1, 2, 0, 3, 4).reshape(B, H, S, out.shape[-1])


def _merge_heads(o_da, o_mla, subln_w, lambda_init):
    o_da = _rmsnorm(o_da, subln_w) * (1.0 - lambda_init)
    B, _, T, _ = o_da.shape
    o_da = o_da.transpose(0, 2, 1, 3).reshape(B, T, DA_WIDTH)
    o_mla = o_mla.transpose(0, 2, 1, 3).reshape(B, T, MLA_WIDTH)
    return jnp.concatenate([o_da, o_mla], axis=-1)


def _conv_ffn(h, w_up, conv_w, conv_b, w_down):
    g, u = jnp.split(h @ w_up, 2, axis=-1)
    T = g.shape[1]
    pad = CONV_W // 2
    gp = jnp.pad(g, ((0, 0), (pad, pad), (0, 0)))
    g = sum(gp[:, j:j + T] * conv_w[j] for j in range(CONV_W)) + conv_b
    return (jax.nn.silu(g) * u) @ w_down


def setup_inputs(seed: int = 0) -> dict:
    key = jax.random.key(seed)
    ks = jax.random.split(key, 24)
    f32 = jnp.float32

    def nrm(k, shape, fan_in, gain=1.0):
        return (gain * fan_in ** -0.5) * jax.random.normal(k, shape, f32)

    def gain(k, shape):
        return 1.0 + 0.05 * jax.random.normal(k, shape, f32)

    L = DEPTH
    return {
        "x": jax.random.normal(ks[0], (BATCH, SEQ, D_MODEL), f32),
        "c": jax.random.normal(ks[1], (BATCH, D_MODEL), f32),
        "ctx": jax.random.normal(ks[2], (BATCH, CTX_LEN, D_MODEL), f32),
        "c_ctx": jax.random.normal(ks[3], (D_MODEL,), f32),
        "w_ada": nrm(ks[4], (L, D_MODEL, N_MOD * D_MODEL), D_MODEL, 0.5),
        "b_ada": 0.02 * jax.random.normal(ks[5], (L, N_MOD * D_MODEL), f32),
        "norm1_w": gain(ks[6], (L, D_MODEL)),
        "w_in": nrm(ks[7], (L, D_MODEL, IN_WIDTH), D_MODEL),
        "q_norm_w": gain(ks[8], (L, Q_RANK)),
        "kv_norm_w": gain(ks[9], (L, KV_RANK)),
        "w_uq": nrm(ks[10], (L, Q_RANK, MLA_HEADS * MLA_QK), Q_RANK),
        "w_ukv": nrm(ks[11], (L, KV_RANK, MLA_HEADS * (MLA_NOPE + MLA_V)), KV_RANK),
        "lambda_q1": 0.1 * jax.random.normal(ks[12], (L, DA_HALF), f32),
        "lambda_k1": 0.1 * jax.random.normal(ks[13], (L, DA_HALF), f32),
        "lambda_q2": 0.1 * jax.random.normal(ks[14], (L, DA_HALF), f32),
        "lambda_k2": 0.1 * jax.random.normal(ks[15], (L, DA_HALF), f32),
        "subln_w": gain(ks[16], (L, HEAD_DIM)),
        "w_o": nrm(ks[17], (L, MIX_WIDTH, D_MODEL), MIX_WIDTH),
        "norm2_w": gain(ks[18], (L, D_MODEL)),
        "w_up": nrm(ks[19], (L, D_MODEL, 2 * D_FF), D_MODEL),
        "conv_w": nrm(ks[20], (L, CONV_W, D_FF), CONV_W),
        "conv_b": 0.02 * jax.random.normal(ks[21], (L, D_FF), f32),
        "w_down": nrm(ks[22], (L, D_FF, D_MODEL), D_FF),
        "final_w": gain(ks[23], (D_MODEL,)),
    }


def reference(x, c, ctx, c_ctx, w_ada, b_ada, norm1_w, w_in, q_norm_w, kv_norm_w, w_uq, w_ukv,
              lambda_q1, lambda_k1, lambda_q2, lambda_k2, subln_w, w_o, norm2_w, w_up,
              conv_w, conv_b, w_down, final_w):
    B, S, D = x.shape
    tabs = _axial_rope_tables(S)
    xc = jnp.broadcast_to(ctx, ctx.shape)
    sc_ = jax.nn.silu(c)
    scc = jax.nn.silu(c_ctx)
    for l in range(DEPTH):
        mod_x = (sc_ @ w_ada[l] + b_ada[l])[:, None, :]
        mod_c = (scc @ w_ada[l] + b_ada[l])[None, None, :]
        sh1, s1, g1, sh2, s2, g2 = jnp.split(mod_x, N_MOD, axis=-1)
        sh1c, s1c, g1c, sh2c, s2c, g2c = jnp.split(mod_c, N_MOD, axis=-1)
        lambda_init = 0.8 - 0.6 * math.exp(-0.3 * l)
        lam = (jnp.exp(jnp.sum(lambda_q1[l].astype(jnp.float32) * lambda_k1[l].astype(jnp.float32)))
               - jnp.exp(jnp.sum(lambda_q2[l].astype(jnp.float32) * lambda_k2[l].astype(jnp.float32)))
               + lambda_init)
        proj = (w_in[l], q_norm_w[l], kv_norm_w[l], w_uq[l], w_ukv[l])

        h = _modulate(_rmsnorm(x, norm1_w[l]), sh1, s1)
        hc = _modulate(_rmsnorm(xc, norm1_w[l]), sh1c, s1c)
        q_da, k_da, v_da, q_mla, k_mla, v_mla = _mixer_inputs(h, tabs, *proj)
        qc_da, kc_da, vc_da, qc_mla, kc_mla, vc_mla = _mixer_inputs(hc, None, *proj)
        k_da_all = jnp.concatenate([kc_da, k_da], axis=2)
        v_da_all = jnp.concatenate([vc_da, v_da], axis=2)
        k_mla_all = jnp.concatenate([kc_mla, k_mla], axis=2)
        v_mla_all = jnp.concatenate([vc_mla, v_mla], axis=2)
        o_da = _sweep(lambda qb: _diff_attend(qb, k_da_all, v_da_all, lam), q_da)
        o_mla = _sweep(lambda qb: _softmax_attend(qb, k_mla_all, v_mla_all, MLA_SCALE), q_mla)
        x = x + g1 * (_merge_heads(o_da, o_mla, subln_w[l], lambda_init) @ w_o[l])

        h2 = _modulate(_rmsnorm(x, norm2_w[l]), sh2, s2)
        x = x + g2 * _conv_ffn(h2, w_up[l], conv_w[l], conv_b[l], w_down[l])

        if l < DEPTH - 1:
            oc_da = _diff_attend(qc_da, kc_da, vc_da, lam)
            oc_mla = _softmax_attend(qc_mla, kc_mla, vc_mla, MLA_SCALE)
            xc = xc + g1c * (_merge_heads(oc_da, oc_mla, subln_w[l], lambda_init) @ w_o[l])
            hc2 = _modulate(_rmsnorm(xc, norm2_w[l]), sh2c, s2c)
            xc = xc + g2c * _conv_ffn(hc2, w_up[l], conv_w[l], conv_b[l], w_down[l])
    return _rmsnorm(x, final_w)
```

```python
import math
import os
import numpy as np
import concourse.bass as bass
import concourse.mybir as mybir
from concourse.bass_utils import run_bass_kernel_spmd

F32 = mybir.dt.float32
BF16 = mybir.dt.bfloat16
AF = mybir.ActivationFunctionType
ALU = mybir.AluOpType

D = 2048
KC = 16
SEQ = 2048
CTX = 256
NQ = 1026
NH = 2306
NK = 2304
DFF = 5632
FC = 44
EPS = 1e-6
LAMBDA_INIT = 0.8 - 0.6 * math.exp(0.0)
DA_SCALE = 1.0 / math.sqrt(64.0)
MLA_SCALE = 1.0 / math.sqrt(192.0)
SEG = 11
NSEG = 4


def ktile_col(t):
    if t < 8:
        return 128 * t
    if t < 16:
        return 1026 + 128 * (t - 8)
    return 2050 + 128 * (t - 16)


KBLOCKS = [(0, 0, 512, 0), (512, 512, 512, 512), (1026, 1024, 512, 1026), (1538, 1536, 512, 1538),
           (2050, 2048, 256, None)]


class Slot:
    __slots__ = ("w", "r")

    def __init__(self):
        self.w = {}
        self.r = {}


def _tadd(d, tok):
    if d.get(tok[0], 0) < tok[1]:
        d[tok[0]] = tok[1]


class Eng:
    def __init__(self, K, eng, name):
        self.K = K
        self.eng = eng
        self.name = name
        self.seen = {}
        self.own = set()
        self.epoch = 0
        self.new_epoch()

    def new_epoch(self):
        self.sem = self.K.new_sem("%s_e%d" % (self.name, self.epoch))
        self.own.add(self.sem)
        self.epoch += 1
        self.n = 0

    def wait(self, toks):
        best = {}
        for (sem, val) in toks:
            if best.get(sem, 0) < val:
                best[sem] = val
        for sem, val in best.items():
            if self.seen.get(sem, 0) >= val:
                continue
            self.eng.wait_ge(sem, val)
            self.seen[sem] = val

    def begin(self, reads=(), writes=(), adds=(), xadds=()):
        toks = []
        for s in reads:
            toks += list(s.w.items())
        for s in writes:
            toks += list(s.w.items())
            toks += list(s.r.items())
        for s in adds:
            toks += list(s.r.items())
        for s in xadds:
            toks += [(sm, v) for (sm, v) in s.w.items() if sm not in self.own]
        self.wait(toks)

    def end(self, ins, reads=(), writes=(), adds=()):
        self.n += 1
        ins.then_inc(self.sem, 1)
        tok = (self.sem, self.n)
        for s in reads:
            _tadd(s.r, tok)
        for s in writes:
            s.w = {tok[0]: tok[1]}
            s.r = {}
        for s in adds:
            _tadd(s.w, tok)
        return tok


class DmaSem:
    def __init__(self, K, name):
        self.sem = K.new_sem(name)
        self.n = 0
        K.dsems.append(self)


class Builder:
    def __init__(self):
        self.nc = bass.Bass("TRN2", target_bir_lowering=False)
        self.sems = []
        self.cur = None
        self.dsems = []

    def new_sem(self, name):
        g = self.nc.semaphore(name)
        s = g.__enter__()
        self.sems.append(g)
        return s

    def region(self, start, size):
        self.cur = [start, start, start + size]

    def sb(self, name, shape, dt):
        size = int(np.prod(shape[1:])) * (4 if dt == F32 else 2)
        size = (size + 63) // 64 * 64
        off = self.cur[1]
        assert off + size <= self.cur[2], (name, off, size, self.cur)
        self.cur[1] += size
        return self.nc.alloc_sbuf_tensor_at(name, list(shape), dt, offset=off)


BASE = 16384
R_PERSIST = (BASE, 6144)
R_HT = (BASE + 6144, 73792)
R_OT = (R_HT[0] + R_HT[1], 32832)
R_REST = (R_OT[0] + R_OT[1], 229376 - (R_OT[0] + R_OT[1]))
A_LAT = R_REST[0]
A_TAB = A_LAT + 20032
A_ROPE = A_TAB + 8320
A_STG = A_ROPE + 4096
A_ATT = A_STG + 24576


class _Stop(Exception):
    pass


def build_program(limit=None):
    K = Builder()
    try:
        _emit(K, limit)
    except _Stop:
        pass
    return K


def _emit(K, limit):
    nc = K.nc

    def din(name, shape):
        return nc.dram_tensor(name, list(shape), F32, kind="ExternalInput").ap()

    xall = din("xall", [NH, D])
    ident_d = din("ident", [128, 128])
    cvec_d = din("cvec", [128, KC, 2])
    w_ada = din("w_ada", [D, 6 * D])
    b_adaT_d = din("b_adaT", [128, 96])
    b_ada_row = din("b_ada_row", [6 * D])
    norm1T_d = din("norm1T", [128, KC])
    norm2T_d = din("norm2T", [128, KC])
    final_w_d = din("final_w", [D])
    w_in = din("w_in", [D, 3776])
    w_uq = din("w_uq", [384, 1536])
    w_ukv = din("w_ukv", [256, 2048])
    qnwT_d = din("qnwT", [128, 3])
    kvnwT_d = din("kvnwT", [128, 2])
    lamb_d = din("lamb", [4 * 64])
    sublnT_d = din("sublnT", [128, 1])
    w_o = din("w_o", [D, D])
    w_up = din("w_up", [D, 2 * DFF])
    w_down = din("w_down", [DFF, D])
    convT_d = din("convT", [128, FC, 4])
    cosT_d = din("cosT", [128, 2050])
    sinT_d = din("sinT", [128, 2050])
    masks_d = din("masks", [128, 2])
    y = nc.dram_tensor("y", [1024, D], F32, kind="ExternalOutput").ap()
    gscr = nc.dram_tensor("gscr", [2, D], F32, kind="Internal").ap()

    PE = Eng(K, nc.tensor, "pe")
    ACT = Eng(K, nc.scalar, "act")
    DVE = Eng(K, nc.vector, "dve")
    PO = Eng(K, nc.gpsimd, "pool")
    SP = Eng(K, nc.sync, "sp")
    ENGS = [PE, ACT, DVE, PO, SP]

    def barrier():
        toks = [(e.sem, e.n) for e in ENGS if e.n > 0] + [(d.sem, d.n) for d in K.dsems if d.n > 0]
        for e in ENGS:
            e.wait(toks)

    def epoch():
        barrier()
        for e in ENGS:
            e.new_epoch()

    dumps = {}

    def stop(name):
        if limit != name:
            return
        barrier()
        ds = DmaSem(K, "dump_s")
        for nm, (t, shape, dt) in dumps.items():
            o = nc.dram_tensor("dbg_" + nm, list(shape), dt, kind="ExternalOutput").ap()
            ins = nc.sync.dma_start(out=o, in_=t)
            ds.n += 16
            ins.then_inc(ds.sem, 16)
        nc.sync.wait_ge(ds.sem, ds.n)
        raise _Stop()

    def dma(E, dsem, out, in_, reads=(), writes=(), adds=()):
        E.begin(reads, writes, adds)
        ins = E.eng.dma_start(out=out, in_=in_)
        dsem.n += 16
        ins.then_inc(dsem.sem, 16)
        tok = (dsem.sem, dsem.n)
        for s in reads:
            _tadd(s.r, tok)
        for s in writes:
            s.w = {tok[0]: tok[1]}
            s.r = {}
        for s in adds:
            _tadd(s.w, tok)
        return tok

    def op(E, fn, reads=(), writes=(), adds=()):
        E.begin(reads, writes, adds)
        ins = fn(E.eng)
        return E.end(ins, reads, writes, adds)

    banks = [nc.alloc_psum_tensor("bank%d" % i, [128, 512], F32) for i in range(8)]
    bslot = [Slot() for _ in range(8)]

    def bank_bf(i):
        return banks[i][:].bitcast(BF16)

    K.region(*R_PERSIST)
    ident = K.sb("ident", [128, 128], BF16)
    ones = K.sb("ones", [128, 128], BF16)
    modT = K.sb("modT", [128, 96, 2], F32)
    b_adaT = K.sb("b_adaT", [128, 96], F32)
    norm1T = K.sb("norm1T", [128, KC], F32)
    norm2T = K.sb("norm2T", [128, KC], F32)
    a1 = K.sb("a1", [128, KC], F32)
    a1c = K.sb("a1c", [128, KC], F32)
    a2 = K.sb("a2", [128, KC], F32)
    qnwT = K.sb("qnwT", [128, 3], F32)
    kvnwT = K.sb("kvnwT", [128, 2], F32)
    sublnT = K.sb("sublnT", [128, 1], F32)
    sw = K.sb("sw", [128, 1], F32)
    lamb = K.sb("lamb", [128, 256], F32)
    lwork = K.sb("lwork", [128, 8], F32)
    neglam = K.sb("neglam", [128, 1], F32)
    masks = K.sb("masks", [128, 2], F32)
    ss = K.sb("ss", [128, 64], F32)
    rstd = K.sb("rstd", [128, 64], F32)
    convT = K.sb("convT", [128, FC, 4], F32)
    cvec = K.sb("cvec", [128, KC, 2], F32)
    scT = K.sb("scT", [128, KC, 2], BF16)
    eps_t = K.sb("eps_t", [128, 1], F32)
    s_const = Slot()
    s_modT = Slot()
    s_modT2 = Slot()
    s_der = Slot()
    s_der2 = Slot()
    s_lam = Slot()
    s_ss = [Slot() for _ in range(64)]
    s_rstd = [Slot() for _ in range(64)]

    cs = DmaSem(K, "cs")
    cs2 = DmaSem(K, "cs2")
    for (dst, src) in [(b_adaT[:], b_adaT_d), (norm1T[:], norm1T_d), (norm2T[:], norm2T_d), (qnwT[:], qnwT_d),
                       (kvnwT[:], kvnwT_d), (sublnT[:], sublnT_d), (masks[:], masks_d), (convT[:], convT_d),
                       (cvec[:], cvec_d), (lamb[:], lamb_d.partition_broadcast(128))]:
        dma(SP, cs, dst, src, adds=[s_const])
    dma(PO, cs2, ident[:], ident_d, adds=[s_const])
    op(DVE, lambda e: e.memset(ones[:], 1.0), adds=[s_const])
    op(DVE, lambda e: e.memset(eps_t[:], EPS), adds=[s_const])

    s_lw = Slot()
    op(DVE, lambda e: e.tensor_tensor(out=lamb[:, 0:64], in0=lamb[:, 0:64], in1=lamb[:, 64:128], op=ALU.mult),
       reads=[s_const], writes=[s_lw])
    op(DVE, lambda e: e.tensor_tensor(out=lamb[:, 128:192], in0=lamb[:, 128:192], in1=lamb[:, 192:256], op=ALU.mult),
       reads=[s_const, s_lw], adds=[s_lw])
    op(DVE, lambda e: e.reduce_sum(out=lwork[:, 0:1], in_=lamb[:, 0:64], axis=mybir.AxisListType.X),
       reads=[s_lw], writes=[s_lam])
    op(DVE, lambda e: e.reduce_sum(out=lwork[:, 1:2], in_=lamb[:, 128:192], axis=mybir.AxisListType.X),
       reads=[s_lw], adds=[s_lam])
    s_lam2 = Slot()
    op(ACT, lambda e: e.activation(out=lwork[:, 2:4], in_=lwork[:, 0:2], func=AF.Exp), reads=[s_lam], writes=[s_lam2])
    s_lam3 = Slot()
    op(DVE, lambda e: e.scalar_tensor_tensor(out=neglam[:], in0=lwork[:, 3:4], scalar=-LAMBDA_INIT, in1=lwork[:, 2:3],
                                             op0=ALU.add, op1=ALU.subtract), reads=[s_lam2], writes=[s_lam3])
    op(DVE, lambda e: e.tensor_scalar(out=sw[:], in0=sublnT[:], scalar1=(1.0 - LAMBDA_INIT), scalar2=None, op0=ALU.mult),
       reads=[s_const], adds=[s_lam3])

    dumps["lw"] = (lwork[:], [128, 8], F32)
    dumps["neglam"] = (neglam[:], [128, 1], F32)
    stop("const")
    op(ACT, lambda e: e.activation(out=scT[:], in_=cvec[:], func=AF.Silu), reads=[s_const], writes=[s_der])
    op(DVE, lambda e: e.memset(banks[7][:, 0:192], 0.0), writes=[bslot[7]])
    gs = DmaSem(K, "gs")
    s_gscr = Slot()
    K.region(*R_OT)
    wstA = [K.sb("wadaA%d" % i, [128, 2048], BF16) for i in range(8)]
    K.region(A_ATT + 38976, 4096)
    wstB = [K.sb("wadaB%d" % i, [128, 512], BF16) for i in range(4)]
    WSET = {"A": (wstA, [Slot() for _ in range(8)], [DmaSem(K, "wadaA_s%d" % i) for i in range(8)]),
            "B": (wstB, [Slot() for _ in range(4)], [DmaSem(K, "wadaB_s%d" % i) for i in range(4)])}
    mod_list = []
    for grp in range(6):
        for k in range(KC):
            if grp < 2:
                mod_list.append((grp, k, 0, 2048, "A"))
            else:
                for pc in range(4):
                    mod_list.append((grp, k, pc * 512, 512, "B"))
    mstate = {"dma": 0, "mm": 0, "A_dma": 0, "A_mm": 0, "B_dma": 0, "B_mm": 0, "firstA": True, "firstB": True}

    def mod_dma_ok():
        i = mstate["dma"]
        if i >= len(mod_list):
            return False
        st = mod_list[i][4]
        return mstate[st + "_dma"] - len(WSET[st][0]) < mstate[st + "_mm"]

    def mod_emit_dma():
        i = mstate["dma"]
        grp, k, c0, w, st = mod_list[i]
        bufs, slots, dsems = WSET[st]
        b = mstate[st + "_dma"] % len(bufs)
        dma(PO, dsems[b], bufs[b][:, 0:w], w_ada[k * 128:(k + 1) * 128, grp * D + c0:grp * D + c0 + w],
            writes=[slots[b]])
        mstate[st + "_dma"] += 1
        mstate["dma"] += 1

    def mod_emit_mm():
        i = mstate["mm"]
        grp, k, c0, w, st = mod_list[i]
        bufs, slots, dsems = WSET[st]
        b = mstate[st + "_mm"] % len(bufs)
        mb = 7 if st == "A" else 6
        PE.begin(reads=[slots[b], s_der], adds=[bslot[mb]], xadds=[bslot[mb]])
        if mstate["first" + st]:
            PE.wait(list(bslot[mb].w.items()))
            mstate["first" + st] = False
        for jj in range(w // 128):
            col = (grp * 16 + c0 // 128 + jj) * 2 if st == "A" else 256 + ((grp - 2) * 16 + c0 // 128 + jj) * 2
            ins = nc.tensor.matmul(banks[mb][:, col:col + 2], lhsT=bufs[b][:, jj * 128:(jj + 1) * 128],
                                   rhs=scT[:, k, :], start=False, stop=(k == KC - 1), skip_group_check=True)
        PE.end(ins, reads=[slots[b], s_der], adds=[bslot[mb]])
        mstate[st + "_mm"] += 1
        mstate["mm"] += 1
        last_of_group = (k == KC - 1) and (c0 + w == 2048)
        if last_of_group:
            for c in range(2):
                mb = 7 if grp < 2 else 6
                pc0 = grp * 32 if grp < 2 else 256 + (grp - 2) * 32
                pv = banks[mb][:, pc0:pc0 + 32].rearrange("p (j c) -> p j c", c=2)[:, :, c]
                op(DVE, lambda e, pv=pv, c=c, grp=grp: e.tensor_tensor(out=modT[:, grp * 16:(grp + 1) * 16, c], in0=pv,
                                                                          in1=b_adaT[:, grp * 16:(grp + 1) * 16], op=ALU.add),
                   reads=[s_const], writes=[bslot[mb]], adds=[s_modT if grp < 2 else s_modT2])
            if grp == 1:
                for (dst, cc) in ((a1, 0), (a1c, 1)):
                    op(DVE, lambda e, dst=dst, cc=cc: e.tensor_scalar(out=dst[:], in0=modT[:, 16:32, cc], scalar1=1.0,
                                                                      scalar2=None, op0=ALU.add),
                       reads=[s_modT, s_const], adds=[s_der])
                    op(DVE, lambda e, dst=dst: e.tensor_tensor(out=dst[:], in0=dst[:], in1=norm1T[:], op=ALU.mult),
                       reads=[s_modT, s_const, s_der], adds=[s_der])
            if grp == 5:
                op(DVE, lambda e: e.tensor_scalar(out=a2[:], in0=modT[:, 64:80, 0], scalar1=1.0, scalar2=None,
                                                  op0=ALU.add), reads=[s_modT2, s_const], adds=[s_der2])
                op(DVE, lambda e: e.tensor_tensor(out=a2[:], in0=a2[:], in1=norm2T[:], op=ALU.mult),
                   reads=[s_modT2, s_const, s_der2], adds=[s_der2])
                with nc.allow_non_contiguous_dma(reason="tiny feature-major -> row scatter"):
                    dma(SP, gs, gscr[0].rearrange("(c p) -> p c", p=128), modT[:, 32:48, 0], reads=[s_modT2],
                        adds=[s_gscr])
                    dma(SP, gs, gscr[1].rearrange("(c p) -> p c", p=128), modT[:, 80:96, 0], reads=[s_modT2],
                        adds=[s_gscr])

    def pump_dma_only():
        while mod_dma_ok():
            mod_emit_dma()

    def pump(n, upto=None):
        for _ in range(n):
            lim = len(mod_list) if upto is None else upto
            if mstate["mm"] >= lim:
                return
            while mod_dma_ok() and mstate["dma"] < lim:
                mod_emit_dma()
            mod_emit_mm()
            while mod_dma_ok() and mstate["dma"] < lim:
                mod_emit_dma()

    pump(2, upto=32)
    dumps["modT"] = (modT[:], [128, 96, 2], F32)
    dumps["a1"] = (a1[:], [128, KC], F32)
    stop("mod")

    K.region(*R_HT)
    hT = K.sb("hT", [128, KC, NH], BF16)
    K.region(*R_OT)
    oT = K.sb("oT", [128, 16, NQ], BF16)
    K.region(A_LAT, 20032 + 8320)
    cqT = K.sb("cqT", [128, 3, NQ], BF16)
    ckvT = K.sb("ckvT", [128, 2, NK], BF16)
    kropeT = K.sb("kropeT", [128, NK], BF16)
    cosT = K.sb("cosT", [128, 2050], BF16)
    sinT = K.sb("sinT", [128, 2050], BF16)
    s_hT = Slot()
    s_oT = Slot()
    s_cq = Slot()
    s_ckv = Slot()
    s_krope = Slot()
    s_tab = Slot()
    tabs = DmaSem(K, "tabs")
    dma(PO, tabs, cosT[:], cosT_d, adds=[s_tab])
    dma(PO, tabs, sinT[:], sinT_d, adds=[s_tab])

    K.region(A_ROPE, 4096 + 24576)
    ropet = [K.sb("ropet%d" % i, [128, 512], F32) for i in range(2)]
    ropet_s = [Slot() for _ in range(2)]

    def rope_evac(bk, np_, width, tcol, outs):
        src = banks[bk][:np_, 0:width]
        t1, t2 = ropet[0][:np_, 0:width], ropet[1][:np_, 0:width]
        op(DVE, lambda e: e.tensor_tensor(out=t1, in0=src, in1=cosT[:np_, tcol:tcol + width], op=ALU.mult),
           reads=[bslot[bk], s_tab], writes=[ropet_s[0]])
        first = True
        for g in range(np_ // 32):
            pin = slice(32 * g, 32 * g + 32)
            gout = g + 1 if g % 2 == 0 else g - 1
            pout = slice(32 * gout, 32 * gout + 32)
            fn = lambda e, pin=pin, pout=pout: e.tensor_tensor(out=ropet[1][pout, 0:width], in0=banks[bk][pin, 0:width],
                                                               in1=sinT[pin, tcol:tcol + width], op=ALU.mult)
            if first:
                op(DVE, fn, reads=[bslot[bk], s_tab], writes=[ropet_s[1]])
                first = False
            else:
                op(DVE, fn, reads=[bslot[bk], s_tab], adds=[ropet_s[1]])
        for (plo, phi, clo, chi, dst, dslot) in outs:
            op(DVE, lambda e, plo=plo, phi=phi, clo=clo, chi=chi, dst=dst: e.tensor_tensor(
                out=dst, in0=ropet[0][plo:phi, clo:chi], in1=ropet[1][plo:phi, clo:chi], op=ALU.add),
               reads=[ropet_s[0], ropet_s[1]], adds=[dslot])

    stg = K.sb("stg", [128, KC, 768], BF16)
    s_stg = Slot()
    stg_d = DmaSem(K, "stg_d")

    def load_cols(dst_c0, src, c0, width, first):
        sv = src.rearrange("(k p) n -> p k n", p=128)
        for q in range(4):
            if first and q == 0:
                dma(PO, stg_d, stg[:, 4 * q:4 * q + 4, dst_c0:dst_c0 + width], sv[:, 4 * q:4 * q + 4, c0:c0 + width],
                    writes=[s_stg])
            else:
                dma(PO, stg_d, stg[:, 4 * q:4 * q + 4, dst_c0:dst_c0 + width], sv[:, 4 * q:4 * q + 4, c0:c0 + width],
                    adds=[s_stg])

    load_cols(0, w_in, 3072, 704, True)

    K.region(A_ATT, 36864)
    xs = [K.sb("xs%d" % i, [128, D], F32) for i in range(3)]
    xnb = [K.sb("xnb%d" % i, [128, D], BF16) for i in range(2)]
    NJ = {"junk": K.sb("junk", [128, D], BF16)}
    xs_s = [Slot() for _ in range(3)]
    xn_s = [Slot() for _ in range(2)]
    xs_d = [DmaSem(K, "xs_d%d" % i) for i in range(3)]
    s_junk = Slot()
    tiles = [(128 * i, 128, 128 * i, False) for i in range(8)] + [(1024, 2, 1024, False)] + \
            [(1026 + 128 * i, 128, 1026 + 128 * i, False) for i in range(8)] + \
            [(2050 + 128 * i, 128, 2050 + 128 * i, True) for i in range(2)]

    def norm_tile(src, src_slot, np_, si, xn, xn_slot, inv_d):
        op(ACT, lambda e: e.activation(out=NJ["junk"][:np_, :], in_=src, func=AF.Square, accum_out=ss[:np_, si:si + 1]),
           reads=[src_slot], writes=[s_junk, s_ss[si]])
        op(ACT, lambda e: e.activation(out=rstd[:np_, si:si + 1], in_=ss[:np_, si:si + 1], func=AF.Ln, scale=inv_d,
                                       bias=eps_t[:np_, 0:1]), reads=[s_ss[si], s_const], writes=[s_rstd[si]])
        op(ACT, lambda e: e.activation(out=rstd[:np_, si:si + 1], in_=rstd[:np_, si:si + 1], func=AF.Exp, scale=-0.5),
           writes=[s_rstd[si]])
        op(DVE, lambda e: e.tensor_scalar(out=xn, in0=src, scalar1=rstd[:np_, si:si + 1], scalar2=None, op0=ALU.mult),
           reads=[src_slot, s_rstd[si]], writes=[xn_slot])

    def transpose_tile(xn, xn_slot, np_, col0, av, bv, dst, dst_slot, bk0, mslots):
        for half in range(2):
            bk = bk0 + half
            PE.begin(reads=[xn_slot, s_const], writes=[bslot[bk]])
            for kk in range(8):
                k = half * 8 + kk
                ins = nc.tensor.transpose(out=bank_bf(bk)[:, kk * 128:kk * 128 + np_], in_=xn[:, k * 128:(k + 1) * 128],
                                          identity=ident[:np_, :np_])
            PE.end(ins, reads=[xn_slot, s_const], writes=[bslot[bk]])
            for kk in range(8):
                k = half * 8 + kk
                src = bank_bf(bk)[:, kk * 128:kk * 128 + np_]
                o = dst[:, k, col0:col0 + np_]
                _ev = os.environ.get('P1EV', 'act')
                if _ev == 'none':
                    continue
                if half == 0:
                    op(ACT, lambda e, o=o, src=src, k=k: e.activation(out=o, in_=src, func=AF.Identity,
                                                                      scale=av[:, k:k + 1], bias=bv(k)),
                       reads=[bslot[bk]] + mslots, adds=[dst_slot])
                else:
                    op(DVE, lambda e, o=o, src=src, k=k: e.tensor_scalar(out=o, in0=src, scalar1=av[:, k:k + 1],
                                                                        scalar2=bv(k), op0=ALU.mult, op1=ALU.add),
                       reads=[bslot[bk]] + mslots, adds=[dst_slot])

    def p1_load(i):
        r0, np_, col0, isctx = tiles[i]
        dma(SP, xs_d[i % 3], xs[i % 3][:np_, :], xall[r0:r0 + np_, :], writes=[xs_s[i % 3]])

    def p1_norm(i):
        r0, np_, col0, isctx = tiles[i]
        b = i % 2
        if i + 2 < len(tiles):
            p1_load(i + 2)
        norm_tile(xs[i % 3][:np_, :], xs_s[i % 3], np_, i, xnb[b][:np_, :], xn_s[b], 1.0 / D)

    def p1_tr(i):
        r0, np_, col0, isctx = tiles[i]
        b = i % 2
        xn = xnb[b][:np_, :]
        for half in range(2):
            bk = 2 * b + half
            PE.begin(reads=[xn_s[b], s_const], writes=[bslot[bk]])
            for kk in range(8):
                k = half * 8 + kk
                ins = nc.tensor.transpose(out=bank_bf(bk)[:, kk * 128:kk * 128 + np_], in_=xn[:, k * 128:(k + 1) * 128],
                                          identity=ident[:np_, :np_])
            PE.end(ins, reads=[xn_s[b], s_const], writes=[bslot[bk]])
            src = bank_bf(bk)[:, 0:1024].rearrange("p (k c) -> p k c", c=128)[:, :, 0:np_]
            dst = hT[:, 8 * half:8 * half + 8, col0:col0 + np_]
            if half == 0:
                op(ACT, lambda e: e.activation(out=dst, in_=src, func=AF.Copy), reads=[bslot[bk]], adds=[s_hT])
            else:
                op(DVE, lambda e: e.tensor_copy(out=dst, in_=src), reads=[bslot[bk]], adds=[s_hT])

    p1_load(0)
    p1_load(1)
    p1_norm(0)
    for i in range(len(tiles)):
        if i + 1 < len(tiles):
            p1_norm(i + 1)
        p1_tr(i)
        pump(2, upto=32)
    pump(32, upto=32)
    for k in range(KC):
        kw = {"writes": [s_hT]} if k == 0 else {"adds": [s_hT]}
        op(DVE, lambda e, k=k: e.tensor_scalar(out=hT[:, k, 0:2050], in0=hT[:, k, 0:2050], scalar1=a1[:, k:k + 1],
                                                scalar2=modT[:, k, 0:1], op0=ALU.mult, op1=ALU.add),
           reads=[s_der, s_modT], **kw)
        op(DVE, lambda e, k=k: e.tensor_scalar(out=hT[:, k, 2050:NH], in0=hT[:, k, 2050:NH], scalar1=a1c[:, k:k + 1],
                                                scalar2=modT[:, k, 1:2], op0=ALU.mult, op1=ALU.add),
           reads=[s_der, s_modT], adds=[s_hT])
    epoch()
    dumps["hT"] = (hT[:], [128, KC, NH], BF16)
    stop("P1")

    K.region(A_ATT, 37568)
    latf = [K.sb("latf%d" % i, [128, 3, 512], F32) for i in range(2)]
    latsq = [K.sb("latsq%d" % i, [128, 3, 512], BF16) for i in range(2)]
    latrs = [K.sb("latrs%d" % i, [128, 512], F32) for i in range(2)]
    s_latf = [Slot() for _ in range(2)]
    s_latsq = [Slot() for _ in range(2)]
    s_latrs = [Slot() for _ in range(2)]
    BSET = [(0, 1, 2), (4, 5, 6)]

    def lat_stage1(st, nch, col_off_w, cols, width, krope):
        for c in range(nch):
            bk = BSET[st][c]
            PE.begin(reads=[s_stg, s_hT], writes=[bslot[bk]])
            for k in range(KC):
                ins = nc.tensor.matmul(banks[bk][:, 0:width], lhsT=stg[:, k, col_off_w + c * 128:col_off_w + (c + 1) * 128],
                                       rhs=hT[:, k, cols:cols + width], start=(k == 0), stop=(k == KC - 1))
            PE.end(ins, reads=[s_stg, s_hT], writes=[bslot[bk]])
            kw = {"writes": [s_latf[st]]} if c == 0 else {"adds": [s_latf[st]]}
            op(ACT, lambda e, c=c, bk=bk: e.activation(out=latf[st][:, c, 0:width], in_=banks[bk][:, 0:width],
                                                       func=AF.Copy), reads=[bslot[bk]], **kw)
        op(DVE, lambda e: e.tensor_tensor(out=latsq[st][:, 0:nch, 0:width], in0=latf[st][:, 0:nch, 0:width],
                                          in1=latf[st][:, 0:nch, 0:width], op=ALU.mult),
           reads=[s_latf[st]], writes=[s_latsq[st]])
        if krope is not None:
            kc_, tc = krope
            bk = BSET[st][2]
            PE.begin(reads=[s_stg, s_hT], writes=[bslot[bk]])
            for k in range(KC):
                ins = nc.tensor.matmul(banks[bk][0:64, 0:width], lhsT=stg[:, k, 640:704], rhs=hT[:, k, cols:cols + width],
                                       start=(k == 0), stop=(k == KC - 1))
            PE.end(ins, reads=[s_stg, s_hT], writes=[bslot[bk]])
            if tc is None:
                op(ACT, lambda e: e.activation(out=kropeT[0:64, kc_:kc_ + width], in_=banks[bk][0:64, 0:width],
                                               func=AF.Copy), reads=[bslot[bk]], adds=[s_krope])
            else:
                rope_evac(bk, 64, width, tc, [(0, 64, 0, width, kropeT[0:64, kc_:kc_ + width], s_krope)])

    def lat_stage2(st, nch, width, nfeat, wT, dst, dst_slot, dcol):
        PE.begin(reads=[s_latsq[st], s_const], writes=[bslot[3]])
        for c in range(nch):
            ins = nc.tensor.matmul(banks[3][:, 0:width], lhsT=ones[:], rhs=latsq[st][:, c, 0:width], start=(c == 0),
                                   stop=(c == nch - 1))
        PE.end(ins, reads=[s_latsq[st], s_const], writes=[bslot[3]])
        op(ACT, lambda e: e.activation(out=latrs[st][:, 0:width], in_=banks[3][:, 0:width], func=AF.Ln,
                                       scale=1.0 / nfeat, bias=eps_t[:, 0:1]), reads=[bslot[3], s_const],
           writes=[s_latrs[st]])
        op(ACT, lambda e: e.activation(out=latrs[st][:, 0:width], in_=latrs[st][:, 0:width], func=AF.Exp, scale=-0.5),
           writes=[s_latrs[st]])
        for c in range(nch):
            op(DVE, lambda e, c=c: e.scalar_tensor_tensor(out=dst[:, c, dcol:dcol + width], in0=latf[st][:, c, 0:width],
                                                           scalar=wT[:, c:c + 1], in1=latrs[st][:, 0:width],
                                                           op0=ALU.mult, op1=ALU.mult),
               reads=[s_latf[st], s_latrs[st], s_const], adds=[dst_slot])

    op(DVE, lambda e: e.memset(kropeT[64:128, :], 0.0), adds=[s_krope])
    lblocks = []
    for blk in range(3):
        lblocks.append(((3, 0, blk * 342, 342, None), (3, 342, 384.0, qnwT, cqT, s_cq, blk * 342)))
    for (hc, kc_, w_, tc) in KBLOCKS:
        lblocks.append(((2, 384, hc, w_, (kc_, tc)), (2, w_, 256.0, kvnwT, ckvT, s_ckv, kc_)))
    for i, (a1_, a2_) in enumerate(lblocks):
        lat_stage1(i % 2, *a1_)
        if i > 0:
            lat_stage2((i - 1) % 2, *lblocks[i - 1][1])
    lat_stage2((len(lblocks) - 1) % 2, *lblocks[-1][1])
    epoch()
    dumps["cqT"] = (cqT[:], [128, 3, NQ], BF16)
    dumps["ckvT"] = (ckvT[:], [128, 2, NK], BF16)
    dumps["kropeT"] = (kropeT[0:64, :], [64, NK], BF16)
    stop("P2")

    K.region(A_ATT, 38976)
    KT = K.sb("KT", [128, 2, NK], BF16)
    QB = K.sb("QB", [128, 2, 6, 342], BF16)
    QR = K.sb("QR", [128, 2, NQ], BF16)
    VV = K.sb("VV", [128, 18, 256], BF16)
    PT = [K.sb("PT%d" % i, [128, 342], BF16) for i in range(5)]
    fin_r = K.sb("fin_r", [128, 342], F32)
    fin_t = K.sb("fin_t", [128, 342], F32)
    fin_o = K.sb("fin_o", [128, 171], F32)
    fin_sq = K.sb("fin_sq", [128, 171], BF16)
    fin_rs = K.sb("fin_rs", [128, 171], F32)
    s_KT, s_QB, s_QR, s_VV = Slot(), Slot(), Slot(), Slot()
    PT_s = [Slot() for _ in range(5)]
    s_fr, s_ft, s_fo, s_fsq, s_frs = Slot(), Slot(), Slot(), Slot(), Slot()
    op(DVE, lambda e: e.memset(QB[:], 0.0), writes=[s_QB])
    op(DVE, lambda e: e.memset(QR[64:128, :, :], 0.0), writes=[s_QR])

    pt_ctr = [0]
    sb_ctr = [0]

    NPT = 5
    DEPTH = 2
    NSB = 3
    OVB = [4, 7]
    SMBS = [5, 3]
    DEFER = 12
    PUMP_EVERY = [0]
    PUMP_HOLD = [0]

    def attention(nblk, bw, s_mm, v_of, exp_scale, fin_a, fin_b):
        steps = [(j, t) for j in range(nblk) for t in range(18)]
        sbank = {}

        def emit_s(i):
            j, t = steps[i]
            sbk = sb_ctr[0] % NSB
            sb_ctr[0] += 1
            sbank[i] = sbk
            s_mm(j, t, sbk)

        for i in range(min(DEPTH, len(steps))):
            emit_s(i)
        pending = None
        for i, (j, t) in enumerate(steps):
            sbk = sbank[i]
            p = pt_ctr[0] % NPT
            pt_ctr[0] += 1
            op(ACT, lambda e, p=p, sbk=sbk: e.activation(out=PT[p][:, 0:bw], in_=banks[sbk][:, 0:bw], func=AF.Exp,
                                                          scale=exp_scale), reads=[bslot[sbk]], writes=[PT_s[p]])
            if i + DEPTH < len(steps):
                emit_s(i + DEPTH)
            ovb = OVB[j % len(OVB)]
            smb = SMBS[j % 2]
            pv_mm(j, t, p, ovb, smb, bw, v_of)
            if PUMP_HOLD[0] > 0:
                PUMP_HOLD[0] -= 1
            elif PUMP_EVERY[0] and i % PUMP_EVERY[0] == 0:
                pump(1)
            if t == DEFER and pending is not None:
                fin_b(pending, 1)
            if t == DEFER + 3 and pending is not None:
                fin_b(pending, 2)
                pending = None
            if t == 17:
                fin_a(j, ovb, smb)
                pending = j
        if pending is not None:
            fin_b(pending, 1)
            fin_b(pending, 2)

    def pv_mm(j, t, p, ovb, smb, bw, v_of):
        first = (t == 0)
        last = (t == 17)
        if first:
            PE.begin(reads=[PT_s[p], s_VV, s_const], writes=[bslot[ovb], bslot[smb]])
        else:
            PE.begin(reads=[PT_s[p], s_VV, s_const])
        nc.tensor.matmul(banks[ovb][:, 0:bw], lhsT=v_of(t), rhs=PT[p][:, 0:bw], start=first, stop=last)
        ins = nc.tensor.matmul(banks[smb][:, 0:bw], lhsT=ones[:], rhs=PT[p][:, 0:bw], start=first, stop=last)
        if first:
            PE.end(ins, reads=[PT_s[p], s_VV, s_const], writes=[bslot[ovb], bslot[smb]])
        else:
            PE.end(ins, reads=[PT_s[p], s_VV, s_const], adds=[bslot[ovb], bslot[smb]])

    K.region(A_STG, 24576)
    wuq = K.sb("wuq", [128, 3, 1536], BF16)
    wukv = K.sb("wukv", [128, 2, 2048], BF16)
    K.region(*R_HT)
    x1 = K.sb("x1", [128, 8, D], F32)
    x1h = K.sb("x1h", [128, D], F32)
    s_x1 = [Slot() for _ in range(9)]
    xl = DmaSem(K, "xl")

    def x1t(i):
        return (x1[:, i, :], 128) if i < 8 else (x1h[0:2, :], 2)

    s_wu = Slot()
    wu_d = DmaSem(K, "wu_d")

    def prefetch_mla():
        for i in range(9):
            ap_, np_ = x1t(i)
            dma(SP, xl, ap_, xall[128 * i:128 * i + np_, :], writes=[s_x1[i]], adds=[s_hT])
        for i in range(9):
            s_x1[i].w = {xl.sem: xl.n}
        dma(PO, wu_d, wuq[:], w_uq.rearrange("(k p) n -> p k n", p=128), writes=[s_stg], adds=[s_wu])
        dma(PO, wu_d, wukv[:], w_ukv.rearrange("(k p) n -> p k n", p=128), adds=[s_wu, s_stg])

    def load_pair(pr):
        load_cols(0, w_in, pr * 256, 256, True)
        load_cols(256, w_in, 1024 + pr * 256, 256, False)
        load_cols(512, w_in, 2048 + pr * 256, 256, False)

    load_pair(0)
    op(DVE, lambda e: e.memset(banks[6][:, 256:448], 0.0), writes=[bslot[6]])
    pump_dma_only()
    PUMP_EVERY[0] = 4
    for pr in range(4):
        rb = [0]

        def nextbank():
            b = rb[0] % 4
            rb[0] += 1
            return b
        def v_tile(t):
            bk = nextbank()
            c0 = ktile_col(t)
            PE.begin(reads=[s_stg, s_hT], writes=[bslot[bk]])
            for k in range(KC):
                ins = nc.tensor.matmul(banks[bk][:, 0:256], lhsT=hT[:, k, c0:c0 + 128], rhs=stg[:, k, 512:768],
                                       start=(k == 0), stop=(k == KC - 1))
            PE.end(ins, reads=[s_stg, s_hT], writes=[bslot[bk]])
            if t == 0:
                ACT.begin(writes=[s_VV])
            op(ACT, lambda e: e.activation(out=VV[:, t, :], in_=banks[bk][:, 0:256], func=AF.Copy),
               reads=[bslot[bk]], adds=[s_VV])

        vq = [(lambda t=t: v_tile(t)) for t in range(18)]

        def v_some(n):
            for _ in range(n):
                if vq:
                    vq.pop(0)()

        for hh in range(2):
            for blk in range(3):
                bk = nextbank()
                PE.begin(reads=[s_stg, s_hT], writes=[bslot[bk]])
                for k in range(KC):
                    ins = nc.tensor.matmul(banks[bk][:, 0:342], lhsT=stg[:, k, hh * 128:(hh + 1) * 128],
                                           rhs=hT[:, k, blk * 342:(blk + 1) * 342], start=(k == 0), stop=(k == KC - 1))
                PE.end(ins, reads=[s_stg, s_hT], writes=[bslot[bk]])
                outs = []
                for sub in range(2):
                    j = 2 * blk + sub
                    outs.append((0, 64, sub * 171, (sub + 1) * 171, QB[0:64, hh, j, 0:171], s_QB))
                    outs.append((64, 128, sub * 171, (sub + 1) * 171, QB[64:128, hh, j, 171:342], s_QB))
                if hh == 0 and blk == 0:
                    DVE.begin(writes=[s_QB])
                rope_evac(bk, 128, 342, blk * 342, outs)
                v_some(1)
        for hh in range(2):
            for (hc, kc_, w_, tc) in KBLOCKS:
                bk = nextbank()
                PE.begin(reads=[s_stg, s_hT], writes=[bslot[bk]])
                for k in range(KC):
                    ins = nc.tensor.matmul(banks[bk][:, 0:w_], lhsT=stg[:, k, 256 + hh * 128:256 + (hh + 1) * 128],
                                           rhs=hT[:, k, hc:hc + w_], start=(k == 0), stop=(k == KC - 1))
                PE.end(ins, reads=[s_stg, s_hT], writes=[bslot[bk]])
                if hh == 0 and kc_ == 0:
                    DVE.begin(writes=[s_KT])
                    ACT.begin(writes=[s_KT])
                if tc is None:
                    op(ACT, lambda e, hh=hh, kc_=kc_, w_=w_, bk=bk: e.activation(out=KT[:, hh, kc_:kc_ + w_],
                                                                                 in_=banks[bk][:, 0:w_], func=AF.Copy),
                       reads=[bslot[bk]], adds=[s_KT])
                else:
                    rope_evac(bk, 128, w_, tc, [(0, 128, 0, w_, KT[:, hh, kc_:kc_ + w_], s_KT)])
                v_some(1)
        while vq:
            vq.pop(0)()
        if pr < 3:
            load_pair(pr + 1)
        else:
            prefetch_mla()
        PUMP_HOLD[0] = 40
        for hh in range(2):
            head = 2 * pr + hh

            def s_mm(j, t, sbk, hh=hh):
                PE.begin(reads=[s_KT, s_QB], writes=[bslot[sbk]])
                ins = nc.tensor.matmul(banks[sbk][:, 0:342], lhsT=KT[:, hh, t * 128:(t + 1) * 128], rhs=QB[:, hh, j, :],
                                       start=True, stop=True)
                PE.end(ins, reads=[s_KT, s_QB], writes=[bslot[sbk]])

            def v_of(t, hh=hh):
                return VV[:, t, hh * 128:(hh + 1) * 128]

            def fin_a(j, ovb, smb, head=head):
                op(DVE, lambda e: e.reciprocal(out=fin_r[:], in_=banks[smb][:, 0:342]), reads=[bslot[smb]], writes=[s_fr])
                op(DVE, lambda e: e.tensor_tensor(out=fin_t[:], in0=banks[ovb][:, 0:342], in1=fin_r[:], op=ALU.mult),
                   reads=[bslot[ovb], s_fr], writes=[s_ft])
                op(DVE, lambda e: e.scalar_tensor_tensor(out=fin_o[:], in0=fin_t[:, 171:342], scalar=neglam[:, 0:1],
                                                         in1=fin_t[:, 0:171], op0=ALU.mult, op1=ALU.add),
                   reads=[s_ft, s_lam3], writes=[s_fo])
                op(DVE, lambda e: e.tensor_tensor(out=fin_sq[:], in0=fin_o[:], in1=fin_o[:], op=ALU.mult),
                   reads=[s_fo], writes=[s_fsq])

            def fin_b(j, stage, head=head):
                if stage == 1:
                    op(DVE, lambda e: e.memset(banks[6][:, 0:171], 0.0), writes=[bslot[6]])
                    PE.begin(reads=[s_fsq, s_const], adds=[bslot[6]], xadds=[bslot[6]])
                    ins = nc.tensor.matmul(banks[6][:, 0:171], lhsT=ones[:], rhs=fin_sq[:], start=False, stop=True,
                                           skip_group_check=True)
                    PE.end(ins, reads=[s_fsq, s_const], adds=[bslot[6]])
                    return
                op(ACT, lambda e: e.activation(out=fin_rs[:], in_=banks[6][:, 0:171], func=AF.Ln, scale=1.0 / 128.0,
                                               bias=eps_t[:, 0:1]), reads=[bslot[6]], writes=[s_frs])
                op(ACT, lambda e: e.activation(out=fin_rs[:], in_=fin_rs[:], func=AF.Exp, scale=-0.5), writes=[s_frs])
                op(DVE, lambda e: e.scalar_tensor_tensor(out=oT[:, head, j * 171:(j + 1) * 171], in0=fin_o[:],
                                                         scalar=sw[:, 0:1], in1=fin_rs[:], op0=ALU.mult, op1=ALU.mult),
                   reads=[s_fo, s_frs, s_lam3], adds=[s_oT])

            attention(6, 342, s_mm, v_of, DA_SCALE, fin_a, fin_b)
        if pr == 0:
            dumps["KT"] = (KT[:], [128, 2, NK], BF16)
            dumps["QB"] = (QB[:], [128, 2, 6, 342], BF16)
            dumps["VV"] = (VV[:], [128, 18, 256], BF16)
            dumps["oT"] = (oT[:], [128, 16, NQ], BF16)
            stop("DA0")
    epoch()
    stop("DA")


    QN = QB[:].rearrange("p h j c -> p h (j c)")
    PUMP_EVERY[0] = 4
    for pr in range(4):
        rb = [0]

        def nextbank():
            b = rb[0] % 4
            rb[0] += 1
            return b
        for hh in range(2):
            h = 2 * pr + hh
            for blk in range(3):
                bk = nextbank()
                PE.begin(reads=[s_wu, s_cq], writes=[bslot[bk]])
                for k in range(3):
                    ins = nc.tensor.matmul(banks[bk][:, 0:342], lhsT=wuq[:, k, h * 192:h * 192 + 128],
                                           rhs=cqT[:, k, blk * 342:(blk + 1) * 342], start=(k == 0), stop=(k == 2))
                PE.end(ins, reads=[s_wu, s_cq], writes=[bslot[bk]])
                if hh == 0 and blk == 0:
                    ACT.begin(writes=[s_QB])
                    DVE.begin(writes=[s_QR])
                op(ACT, lambda e, hh=hh, blk=blk, bk=bk: e.activation(out=QN[:, hh, blk * 342:(blk + 1) * 342],
                                                                      in_=banks[bk][:, 0:342], func=AF.Copy),
                   reads=[bslot[bk]], adds=[s_QB])
                bk = nextbank()
                PE.begin(reads=[s_wu, s_cq], writes=[bslot[bk]])
                for k in range(3):
                    ins = nc.tensor.matmul(banks[bk][0:64, 0:342], lhsT=wuq[:, k, h * 192 + 128:h * 192 + 192],
                                           rhs=cqT[:, k, blk * 342:(blk + 1) * 342], start=(k == 0), stop=(k == 2))
                PE.end(ins, reads=[s_wu, s_cq], writes=[bslot[bk]])
                rope_evac(bk, 64, 342, blk * 342, [(0, 64, 0, 342, QR[0:64, hh, blk * 342:(blk + 1) * 342], s_QR)])
            for (hc, kc_, w_, tc) in KBLOCKS:
                bk = nextbank()
                PE.begin(reads=[s_wu, s_ckv], writes=[bslot[bk]])
                for k in range(2):
                    ins = nc.tensor.matmul(banks[bk][:, 0:w_], lhsT=wukv[:, k, h * 256:h * 256 + 128],
                                           rhs=ckvT[:, k, kc_:kc_ + w_], start=(k == 0), stop=(k == 1))
                PE.end(ins, reads=[s_wu, s_ckv], writes=[bslot[bk]])
                if hh == 0 and kc_ == 0:
                    DVE.begin(writes=[s_KT])
                    ACT.begin(writes=[s_KT])
                if (kc_ // 512) % 2 == 0:
                    op(ACT, lambda e, hh=hh, kc_=kc_, w_=w_, bk=bk: e.activation(out=KT[:, hh, kc_:kc_ + w_],
                                                                                 in_=banks[bk][:, 0:w_], func=AF.Copy),
                       reads=[bslot[bk]], adds=[s_KT])
                else:
                    op(DVE, lambda e, hh=hh, kc_=kc_, w_=w_, bk=bk: e.tensor_copy(out=KT[:, hh, kc_:kc_ + w_],
                                                                                 in_=banks[bk][:, 0:w_]),
                       reads=[bslot[bk]], adds=[s_KT])
        for t in range(18):
            bk = nextbank()
            PE.begin(reads=[s_wu, s_ckv], writes=[bslot[bk]])
            for hh in range(2):
                h = 2 * pr + hh
                for k in range(2):
                    ins = nc.tensor.matmul(banks[bk][:, hh * 128:(hh + 1) * 128], lhsT=ckvT[:, k, t * 128:(t + 1) * 128],
                                           rhs=wukv[:, k, h * 256 + 128:h * 256 + 256], start=(k == 0), stop=(k == 1))
            PE.end(ins, reads=[s_wu, s_ckv], writes=[bslot[bk]])
            if t == 0:
                DVE.begin(writes=[s_VV])
                ACT.begin(writes=[s_VV])
            if t % 2 == 0:
                op(ACT, lambda e, t=t, bk=bk: e.activation(out=VV[:, t, :], in_=banks[bk][:, 0:256], func=AF.Copy),
                   reads=[bslot[bk]], adds=[s_VV])
            else:
                op(DVE, lambda e, t=t, bk=bk: e.tensor_copy(out=VV[:, t, :], in_=banks[bk][:, 0:256]),
                   reads=[bslot[bk]], adds=[s_VV])
        for hh in range(2):
            head = 8 + 2 * pr + hh

            def s_mm(j, t, sbk, hh=hh):
                PE.begin(reads=[s_KT, s_QB, s_QR, s_krope], writes=[bslot[sbk]])
                nc.tensor.matmul(banks[sbk][:, 0:342], lhsT=KT[:, hh, t * 128:(t + 1) * 128],
                                 rhs=QN[:, hh, j * 342:(j + 1) * 342], start=True, stop=False)
                ins = nc.tensor.matmul(banks[sbk][:, 0:342], lhsT=kropeT[:, t * 128:(t + 1) * 128],
                                       rhs=QR[:, hh, j * 342:(j + 1) * 342], start=False, stop=True)
                PE.end(ins, reads=[s_KT, s_QB, s_QR, s_krope], writes=[bslot[sbk]])

            def v_of(t, hh=hh):
                return VV[:, t, hh * 128:(hh + 1) * 128]

            def fin_a(j, ovb, smb, head=head):
                op(DVE, lambda e: e.reciprocal(out=fin_r[:], in_=banks[smb][:, 0:342]), reads=[bslot[smb]], writes=[s_fr])
                op(DVE, lambda e: e.tensor_tensor(out=oT[:, head, j * 342:(j + 1) * 342], in0=banks[ovb][:, 0:342],
                                                  in1=fin_r[:], op=ALU.mult), reads=[bslot[ovb], s_fr], adds=[s_oT])

            attention(3, 342, s_mm, v_of, MLA_SCALE, fin_a, lambda j, stage: None)
    pump(100000)
    PUMP_EVERY[0] = 0
    epoch()
    stop("MLA")

    K.region(*R_REST)
    h2T = K.sb("h2T", [128, KC, NQ], BF16)
    mC = K.cur[1]
    wo = [K.sb("wo%d" % i, [128, KC, 512], BF16) for i in range(2)]
    g1_b = K.sb("g1_b", [128, D], F32)
    tmpB = [K.sb("tmpB%d" % i, [128, 512], F32) for i in range(2)]
    xn2 = [K.sb("xn2_%d" % i, [128, D], BF16) for i in range(2)]
    NJ["junk"] = K.sb("junkB", [128, D], BF16)
    s_h2T = Slot()
    wo_s = [Slot() for _ in range(2)]
    wo_d = [DmaSem(K, "wo_d%d" % i) for i in range(2)]
    s_g1b = Slot()
    tmpB_s = [Slot() for _ in range(2)]
    xn2_s = [Slot() for _ in range(2)]
    dma(SP, gs, g1_b[:], gscr[0].partition_broadcast(128), reads=[s_gscr], writes=[s_g1b])
    wov = w_o.rearrange("(k p) n -> p k n", p=128)
    ev = [0]
    def load_wo(n):
        b = n % 2
        for q in range(4):
            if q == 0:
                dma(PO, wo_d[b], wo[b][:, 0:4, :], wov[:, 0:4, n * 512:(n + 1) * 512], writes=[wo_s[b]])
            else:
                dma(PO, wo_d[b], wo[b][:, 4 * q:4 * q + 4, :], wov[:, 4 * q:4 * q + 4, n * 512:(n + 1) * 512],
                    adds=[wo_s[b]])

    def b2_norm(i):
        ap_, np_ = x1t(i)
        b = i % 2
        norm_tile(ap_, s_x1[i], np_, 20 + i, xn2[b][:np_, :], xn2_s[b], 1.0 / D)

    def b2_tr(i):
        ap_, np_ = x1t(i)
        b = i % 2
        transpose_tile(xn2[b][:np_, :], xn2_s[b], np_, 128 * i, a2, (lambda k: modT[:, 48 + k, 0:1]), h2T, s_h2T,
                       4 + 2 * b, [s_der2, s_modT2])

    load_wo(0)
    for n in range(4):
        b = n % 2
        if n < 3:
            load_wo(n + 1)
        for i in range(9):
            ap_, np_ = x1t(i)
            c0 = 128 * i
            bk = ev[0] % 4
            tb = ev[0] % 2
            ev[0] += 1
            PE.begin(reads=[wo_s[b], s_oT], writes=[bslot[bk]])
            for k in range(KC):
                ins = nc.tensor.matmul(banks[bk][:np_, :], lhsT=oT[:, k, c0:c0 + np_], rhs=wo[b][:, k, :],
                                       start=(k == 0), stop=(k == KC - 1))
            PE.end(ins, reads=[wo_s[b], s_oT], writes=[bslot[bk]])
            op(DVE, lambda e, bk=bk, tb=tb, np_=np_, n=n: e.tensor_tensor(out=tmpB[tb][:np_, :], in0=banks[bk][:np_, :],
                                                                           in1=g1_b[:np_, n * 512:(n + 1) * 512],
                                                                           op=ALU.mult),
               reads=[bslot[bk], s_g1b], writes=[tmpB_s[tb]])
            xa = ap_[:, n * 512:(n + 1) * 512]
            op(DVE, lambda e, xa=xa, tb=tb, np_=np_: e.tensor_tensor(out=xa, in0=xa, in1=tmpB[tb][:np_, :], op=ALU.add),
               reads=[tmpB_s[tb]], writes=[s_x1[i]])
            if n == 3:
                b2_norm(i)
                if i >= 1:
                    b2_tr(i - 1)
    b2_tr(8)
    op(DVE, lambda e: e.tensor_scalar(out=h2T[:, :, 1024:1025], in0=h2T[:, :, 1024:1025], scalar1=masks[:, 0:1],
                                      scalar2=None, op0=ALU.mult), reads=[s_const], writes=[s_h2T])
    op(DVE, lambda e: e.tensor_scalar(out=h2T[:, :, 1025:1026], in0=h2T[:, :, 1025:1026], scalar1=masks[:, 1:2],
                                      scalar2=None, op0=ALU.mult), reads=[s_const], writes=[s_h2T])
    epoch()
    dumps["h2T"] = (h2T[:], [128, KC, NQ], BF16)
    dumps["x1"] = (x1[:], [128, 8, D], F32)
    stop("B")

    K.region(*R_OT)
    aT = K.sb("aT", [128, SEG, 1024], BF16)
    g2_b = K.sb("g2_b", [128, D], F32)
    K.region(mC, R_REST[0] + R_REST[1] - mC)
    mF = K.cur[1]
    wup = [K.sb("wup%d" % i, [128, KC, 256], BF16) for i in range(3)]
    wdn = [K.sb("wdn%d" % i, [128, SEG, 512], BF16) for i in range(2)]
    G = K.sb("G", [128, NQ], F32)
    Tc = K.sb("Tc", [128, 1024], F32)
    Sg = K.sb("Sg", [128, 1024], F32)
    tmpC = [K.sb("tmpC%d" % i, [128, 512], F32) for i in range(2)]
    wup_s = [Slot() for _ in range(3)]
    wup_d = [DmaSem(K, "wup_d%d" % i) for i in range(3)]
    wdn_s = [Slot() for _ in range(2)]
    wdn_d = [DmaSem(K, "wdn_d%d" % i) for i in range(2)]
    s_aT, s_G, s_Tc, s_Sg, s_g2b = Slot(), Slot(), Slot(), Slot(), Slot()
    tmpC_s = [Slot() for _ in range(2)]
    dma(SP, gs, g2_b[:], gscr[1].partition_broadcast(128), reads=[s_gscr], writes=[s_g2b])
    wupv = w_up.rearrange("(k p) n -> p k n", p=128)
    wdnv = w_down.rearrange("(c p) n -> p c n", p=128)
    rbk = [0]
    UB = [(0, 342), (342, 342), (684, 340)]
    wdn_it = [0]
    def load_wup(cg):
        wb = cg % 3
        for q in range(2):
            dma(PO, wup_d[wb], wup[wb][:, 8 * q:8 * q + 8, 0:128], wupv[:, 8 * q:8 * q + 8, cg * 128:(cg + 1) * 128],
                **({"writes": [wup_s[wb]]} if q == 0 else {"adds": [wup_s[wb]]}))
        for q in range(2):
            dma(PO, wup_d[wb], wup[wb][:, 8 * q:8 * q + 8, 128:256],
                wupv[:, 8 * q:8 * q + 8, DFF + cg * 128:DFF + (cg + 1) * 128], adds=[wup_s[wb]])

    def load_wdn(sg, n):
        db = (sg * 4 + n) % 2
        r0 = sg * SEG
        dma(PO, wdn_d[db], wdn[db][:, 0:6, :], wdnv[:, r0:r0 + 6, n * 512:(n + 1) * 512], writes=[wdn_s[db]])
        dma(PO, wdn_d[db], wdn[db][:, 6:SEG, :], wdnv[:, r0 + 6:r0 + SEG, n * 512:(n + 1) * 512], adds=[wdn_s[db]])

    K.region(R_REST[0], 32832)
    fw_b = K.sb("fw_b", [128, D], F32)
    yt = [K.sb("yt%d" % i, [128, D], F32) for i in range(2)]
    junkF = K.sb("junkF", [128, D], BF16)
    s_fw = Slot()
    yt_s = [Slot() for _ in range(2)]
    ys = DmaSem(K, "ys")

    def final_tile(i):
        b = i % 2
        si = 40 + i
        op(ACT, lambda e: e.activation(out=junkF[:], in_=x1[:, i, :], func=AF.Square, accum_out=ss[:, si:si + 1]),
           reads=[s_x1[i]], writes=[s_junk, s_ss[si]])
        op(ACT, lambda e: e.activation(out=rstd[:, si:si + 1], in_=ss[:, si:si + 1], func=AF.Ln, scale=1.0 / D,
                                       bias=eps_t[:, 0:1]), reads=[s_ss[si]], writes=[s_rstd[si]])
        op(ACT, lambda e: e.activation(out=rstd[:, si:si + 1], in_=rstd[:, si:si + 1], func=AF.Exp, scale=-0.5),
           writes=[s_rstd[si]])
        op(DVE, lambda e: e.scalar_tensor_tensor(out=yt[b][:], in0=x1[:, i, :], scalar=rstd[:, si:si + 1],
                                                 in1=fw_b[:], op0=ALU.mult, op1=ALU.mult),
           reads=[s_x1[i], s_rstd[si], s_fw], writes=[yt_s[b]])
        dma(SP, ys, y[128 * i:128 * (i + 1), :], yt[b][:], reads=[yt_s[b]])

    load_wup(0)
    load_wup(1)
    for sg in range(NSEG):
        for c in range(SEG):
            cg = sg * SEG + c
            wb = cg % 3
            if cg + 2 < FC:
                load_wup(cg + 2)
            if c == SEG - 2:
                load_wdn(sg, 0)
            gb = []
            for blk in range(3):
                bk = rbk[0] % 8
                rbk[0] += 1
                gb.append(bk)
                PE.begin(reads=[wup_s[wb], s_h2T], writes=[bslot[bk]])
                for k in range(KC):
                    ins = nc.tensor.matmul(banks[bk][:, 0:342], lhsT=wup[wb][:, k, 0:128],
                                           rhs=h2T[:, k, blk * 342:(blk + 1) * 342], start=(k == 0), stop=(k == KC - 1))
                PE.end(ins, reads=[wup_s[wb], s_h2T], writes=[bslot[bk]])
                if blk == 0:
                    op(ACT, lambda e, bk=bk, blk=blk: e.activation(out=G[:, blk * 342:(blk + 1) * 342],
                                                                   in_=banks[bk][:, 0:342], func=AF.Copy),
                       reads=[bslot[bk]], writes=[s_G])
                else:
                    op(ACT, lambda e, bk=bk, blk=blk: e.activation(out=G[:, blk * 342:(blk + 1) * 342],
                                                                   in_=banks[bk][:, 0:342], func=AF.Copy),
                       reads=[bslot[bk]], adds=[s_G])
            ub = []
            for blk in range(3):
                bk = rbk[0] % 8
                rbk[0] += 1
                ub.append(bk)
                u0, uw = UB[blk]
                PE.begin(reads=[wup_s[wb], s_h2T], writes=[bslot[bk]])
                for k in range(KC):
                    ins = nc.tensor.matmul(banks[bk][:, 0:uw], lhsT=wup[wb][:, k, 128:256], rhs=h2T[:, k, u0:u0 + uw],
                                           start=(k == 0), stop=(k == KC - 1))
                PE.end(ins, reads=[wup_s[wb], s_h2T], writes=[bslot[bk]])
            w0, w1, w2, cb = (convT[:, cg, j:j + 1] for j in range(4))
            op(DVE, lambda e: e.tensor_scalar(out=Tc[:, 0:1024], in0=G[:, 0:1024], scalar1=w1, scalar2=cb, op0=ALU.mult,
                                              op1=ALU.add), reads=[s_G, s_const], writes=[s_Tc])
            op(DVE, lambda e: e.scalar_tensor_tensor(out=Tc[:, 1:1024], in0=G[:, 0:1023], scalar=w0, in1=Tc[:, 1:1024],
                                                     op0=ALU.mult, op1=ALU.add), reads=[s_G], writes=[s_Tc])
            op(DVE, lambda e: e.scalar_tensor_tensor(out=Tc[:, 0:1023], in0=G[:, 1:1024], scalar=w2, in1=Tc[:, 0:1023],
                                                     op0=ALU.mult, op1=ALU.add), reads=[s_G], writes=[s_Tc])
            op(DVE, lambda e: e.scalar_tensor_tensor(out=Tc[:, 0:1], in0=G[:, 1024:1025], scalar=w0, in1=Tc[:, 0:1],
                                                     op0=ALU.mult, op1=ALU.add), reads=[s_G], writes=[s_Tc])
            op(DVE, lambda e: e.scalar_tensor_tensor(out=Tc[:, 1023:1024], in0=G[:, 1025:1026], scalar=w2,
                                                     in1=Tc[:, 1023:1024], op0=ALU.mult, op1=ALU.add),
               reads=[s_G], writes=[s_Tc])
            op(ACT, lambda e: e.activation(out=Sg[:], in_=Tc[:], func=AF.Silu), reads=[s_Tc], writes=[s_Sg])
            for blk in range(3):
                u0, uw = UB[blk]
                bk = ub[blk]
                fn = lambda e, bk=bk, u0=u0, uw=uw, c=c: e.tensor_tensor(out=aT[:, c, u0:u0 + uw], in0=banks[bk][:, 0:uw],
                                                                         in1=Sg[:, u0:u0 + uw], op=ALU.mult)
                if c == 0 and blk == 0:
                    op(DVE, fn, reads=[bslot[bk], s_Sg], writes=[s_aT])
                else:
                    op(DVE, fn, reads=[bslot[bk], s_Sg], adds=[s_aT])
        for n in range(4):
            db = (sg * 4 + n) % 2
            if n < 3:
                load_wdn(sg, n + 1)
            for i in range(8):
                bk = rbk[0] % 8
                rbk[0] += 1
                tb = i % 2
                PE.begin(reads=[wdn_s[db], s_aT], writes=[bslot[bk]])
                for c in range(SEG):
                    ins = nc.tensor.matmul(banks[bk][:, :], lhsT=aT[:, c, i * 128:(i + 1) * 128], rhs=wdn[db][:, c, :],
                                           start=(c == 0), stop=(c == SEG - 1))
                PE.end(ins, reads=[wdn_s[db], s_aT], writes=[bslot[bk]])
                op(DVE, lambda e, bk=bk, tb=tb, n=n: e.tensor_tensor(out=tmpC[tb][:], in0=banks[bk][:, :],
                                                                      in1=g2_b[:, n * 512:(n + 1) * 512], op=ALU.mult),
                   reads=[bslot[bk], s_g2b], writes=[tmpC_s[tb]])
                xa = x1[:, i, n * 512:(n + 1) * 512]
                op(DVE, lambda e, xa=xa, tb=tb: e.tensor_tensor(out=xa, in0=xa, in1=tmpC[tb][:], op=ALU.add),
                   reads=[tmpC_s[tb]], writes=[s_x1[i]])
                if sg == NSEG - 1 and n == 3:
                    if i == 0:
                        dma(SP, gs, fw_b[:], final_w_d.partition_broadcast(128), writes=[s_fw, s_h2T])
                        ACT.begin(writes=[s_h2T])
                        DVE.begin(writes=[s_h2T])
                    final_tile(i)
    epoch()
    stop("C")

    SP.wait([(ys.sem, ys.n)])


_CACHE = {}


def _rope_tables(pos):
    inv = (10000.0 ** (-np.arange(16, dtype=np.float32) / 16.0)).astype(np.float32)
    row = (pos // 64).astype(np.float32)
    col = (pos % 64).astype(np.float32)
    cosT = np.zeros((128, len(pos)), np.float32)
    sinT = np.zeros((128, len(pos)), np.float32)
    for p in range(128):
        i = p % 32
        ang = (row if i < 16 else col) * inv[i % 16]
        cosT[p] = np.cos(ang)
        sinT[p] = np.sin(ang) if (p % 64) < 32 else -np.sin(ang)
    return cosT, sinT


def _perm64():
    return np.concatenate([np.arange(0, 16), np.arange(32, 48), np.arange(16, 32), np.arange(48, 64)])


def kernel(x, c, ctx, c_ctx, w_ada, b_ada, norm1_w, w_in, q_norm_w, kv_norm_w, w_uq, w_ukv,
           lambda_q1, lambda_k1, lambda_q2, lambda_k2, subln_w, w_o, norm2_w, w_up,
           conv_w, conv_b, w_down, final_w):
    if "nc" not in _CACHE:
        _CACHE["nc"] = build_program().nc
    nc = _CACHE["nc"]
    in_maps = _prep(x, c, ctx, c_ctx, w_ada, b_ada, norm1_w, w_in, q_norm_w, kv_norm_w, w_uq, w_ukv,
                    lambda_q1, lambda_k1, lambda_q2, lambda_k2, subln_w, w_o, norm2_w, w_up,
                    conv_w, conv_b, w_down, final_w)
    res = run_bass_kernel_spmd(nc, in_maps, core_ids=list(range(8)))
    out = np.empty((4, SEQ, D), np.float32)
    for core in range(8):
        b, half = core // 2, core % 2
        out[b, half * 1024:(half + 1) * 1024] = res.results[core]["y"]
    return out


def _prep(x, c, ctx, c_ctx, w_ada, b_ada, norm1_w, w_in, q_norm_w, kv_norm_w, w_uq, w_ukv,
          lambda_q1, lambda_k1, lambda_q2, lambda_k2, subln_w, w_o, norm2_w, w_up,
          conv_w, conv_b, w_down, final_w):
    f = lambda a: np.ascontiguousarray(np.asarray(a, dtype=np.float32))
    x, c, ctx, c_ctx = f(x), f(c), f(ctx), f(c_ctx)
    p64 = _perm64()
    w_in0 = f(w_in)[0]
    cols = np.arange(3776)
    for base in (0, 1024):
        for hb in range(16):
            s0 = base + hb * 64
            cols[s0:s0 + 64] = s0 + p64
    cols[3712:3776] = 3712 + p64
    w_in_p = np.ascontiguousarray(w_in0[:, cols])
    w_uq0 = f(w_uq)[0]
    ucols = np.arange(1536)
    for h in range(8):
        s0 = h * 192 + 128
        ucols[s0:s0 + 64] = s0 + p64
    w_uq_p = np.ascontiguousarray(w_uq0[:, ucols])
    fm = lambda v, n: np.ascontiguousarray(f(v).reshape(n, 128).T)
    b_ada0 = f(b_ada)[0]
    common = {
        "ident": np.eye(128, dtype=np.float32),
        "w_ada": f(w_ada)[0], "b_adaT": fm(b_ada0, 96), "b_ada_row": b_ada0,
        "norm1T": fm(f(norm1_w)[0], 16), "norm2T": fm(f(norm2_w)[0], 16), "final_w": f(final_w),
        "w_in": w_in_p, "w_uq": w_uq_p, "w_ukv": f(w_ukv)[0],
        "qnwT": fm(f(q_norm_w)[0], 3), "kvnwT": fm(f(kv_norm_w)[0], 2),
        "lamb": np.concatenate([f(lambda_q1)[0], f(lambda_k1)[0], f(lambda_q2)[0], f(lambda_k2)[0]]),
        "sublnT": f(subln_w)[0].reshape(128, 1).copy(),
        "w_o": f(w_o)[0], "w_up": f(w_up)[0], "w_down": f(w_down)[0],
        "convT": np.ascontiguousarray(np.concatenate([f(conv_w)[0], f(conv_b)], 0).reshape(4, FC, 128).transpose(2, 1, 0)),
    }
    in_maps = []
    for core in range(8):
        b, half = core // 2, core % 2
        lo = half * 1024
        olo = (1 - half) * 1024
        xall = np.zeros((NH, D), np.float32)
        xall[0:1024] = x[b, lo:lo + 1024]
        pos_q = np.zeros(NQ, np.int64)
        pos_q[0:1024] = np.arange(lo, lo + 1024)
        masks = np.zeros((128, 2), np.float32)
        if lo - 1 >= 0:
            xall[1024] = x[b, lo - 1]
            pos_q[1024] = lo - 1
            masks[:, 0] = 1.0
        if lo + 1024 < SEQ:
            xall[1025] = x[b, lo + 1024]
            pos_q[1025] = lo + 1024
            masks[:, 1] = 1.0
        xall[1026:2050] = x[b, olo:olo + 1024]
        xall[2050:2306] = ctx[b]
        pos = np.concatenate([pos_q, np.arange(olo, olo + 1024)])
        cosT, sinT = _rope_tables(pos)
        cvec = np.stack([c[b].reshape(16, 128).T, c_ctx.reshape(16, 128).T], axis=-1)
        m = dict(common)
        m.update({"xall": xall, "cvec": np.ascontiguousarray(cvec, dtype=np.float32), "cosT": cosT, "sinT": sinT,
                  "masks": masks})
        in_maps.append(m)
    return in_maps
```

```python
import math
import os
import numpy as np
import concourse.bass as bass
import concourse.mybir as mybir
from concourse.bass_utils import run_bass_kernel_spmd

F32 = mybir.dt.float32
BF16 = mybir.dt.bfloat16
AF = mybir.ActivationFunctionType
ALU = mybir.AluOpType

D = 2048
KC = 16
SEQ = 2048
CTX = 256
NQ = 1026
NH = 2306
NK = 2304
DFF = 5632
FC = 44
EPS = 1e-6
LAMBDA_INIT = 0.8 - 0.6 * math.exp(0.0)
DA_SCALE = 1.0 / math.sqrt(64.0)
MLA_SCALE = 1.0 / math.sqrt(192.0)
SEG = 11
NSEG = 4


def ktile_col(t):
    if t < 8:
        return 128 * t
    if t < 16:
        return 1026 + 128 * (t - 8)
    return 2050 + 128 * (t - 16)


KBLOCKS = [(0, 0, 512, 0), (512, 512, 512, 512), (1026, 1024, 512, 1026), (1538, 1536, 512, 1538),
           (2050, 2048, 256, None)]


class Slot:
    __slots__ = ("w", "r")

    def __init__(self):
        self.w = {}
        self.r = {}


def _tadd(d, tok):
    if d.get(tok[0], 0) < tok[1]:
        d[tok[0]] = tok[1]


class Eng:
    def __init__(self, K, eng, name):
        self.K = K
        self.eng = eng
        self.name = name
        self.seen = {}
        self.own = set()
        self.epoch = 0
        self.new_epoch()

    def new_epoch(self):
        self.sem = self.K.new_sem("%s_e%d" % (self.name, self.epoch))
        self.own.add(self.sem)
        self.epoch += 1
        self.n = 0

    def wait(self, toks):
        best = {}
        for (sem, val) in toks:
            if best.get(sem, 0) < val:
                best[sem] = val
        for sem, val in best.items():
            if self.seen.get(sem, 0) >= val:
                continue
            self.eng.wait_ge(sem, val)
            self.seen[sem] = val

    def begin(self, reads=(), writes=(), adds=(), xadds=()):
        toks = []
        for s in reads:
            toks += list(s.w.items())
        for s in writes:
            toks += list(s.w.items())
            toks += list(s.r.items())
        for s in adds:
            toks += list(s.r.items())
        for s in xadds:
            toks += [(sm, v) for (sm, v) in s.w.items() if sm not in self.own]
        self.wait(toks)

    def end(self, ins, reads=(), writes=(), adds=()):
        self.n += 1
        ins.then_inc(self.sem, 1)
        tok = (self.sem, self.n)
        for s in reads:
            _tadd(s.r, tok)
        for s in writes:
            s.w = {tok[0]: tok[1]}
            s.r = {}
        for s in adds:
            _tadd(s.w, tok)
        return tok


class DmaSem:
    def __init__(self, K, name):
        self.sem = K.new_sem(name)
        self.n = 0
        K.dsems.append(self)


class Builder:
    def __init__(self):
        self.nc = bass.Bass("TRN2", target_bir_lowering=False)
        self.sems = []
        self.cur = None
        self.dsems = []

    def new_sem(self, name):
        g = self.nc.semaphore(name)
        s = g.__enter__()
        self.sems.append(g)
        return s

    def region(self, start, size):
        self.cur = [start, start, start + size]

    def sb(self, name, shape, dt):
        size = int(np.prod(shape[1:])) * (4 if dt == F32 else 2)
        size = (size + 63) // 64 * 64
        off = self.cur[1]
        assert off + size <= self.cur[2], (name, off, size, self.cur)
        self.cur[1] += size
        return self.nc.alloc_sbuf_tensor_at(name, list(shape), dt, offset=off)


BASE = 16384
R_PERSIST = (BASE, 6144)
R_HT = (BASE + 6144, 73792)
R_OT = (R_HT[0] + R_HT[1], 32832)
R_REST = (R_OT[0] + R_OT[1], 229376 - (R_OT[0] + R_OT[1]))
A_LAT = R_REST[0]
A_TAB = A_LAT + 20032
A_ROPE = A_TAB + 8320
A_STG = A_ROPE + 4096
A_ATT = A_STG + 24576


class _Stop(Exception):
    pass


def build_program(limit=None):
    K = Builder()
    try:
        _emit(K, limit)
    except _Stop:
        pass
    return K


def _emit(K, limit):
    nc = K.nc

    def din(name, shape):
        return nc.dram_tensor(name, list(shape), F32, kind="ExternalInput").ap()

    xall = din("xall", [NH, D])
    ident_d = din("ident", [128, 128])
    cvec_d = din("cvec", [128, KC, 2])
    w_ada = din("w_ada", [D, 6 * D])
    b_adaT_d = din("b_adaT", [128, 96])
    b_ada_row = din("b_ada_row", [6 * D])
    norm1T_d = din("norm1T", [128, KC])
    norm2T_d = din("norm2T", [128, KC])
    final_w_d = din("final_w", [D])
    w_in = din("w_in", [D, 3776])
    w_uq = din("w_uq", [384, 1536])
    w_ukv = din("w_ukv", [256, 2048])
    qnwT_d = din("qnwT", [128, 3])
    kvnwT_d = din("kvnwT", [128, 2])
    lamb_d = din("lamb", [4 * 64])
    sublnT_d = din("sublnT", [128, 1])
    w_o = din("w_o", [D, D])
    w_up = din("w_up", [D, 2 * DFF])
    w_down = din("w_down", [DFF, D])
    convT_d = din("convT", [128, FC, 4])
    cosT_d = din("cosT", [128, 2050])
    sinT_d = din("sinT", [128, 2050])
    masks_d = din("masks", [128, 2])
    y = nc.dram_tensor("y", [1024, D], F32, kind="ExternalOutput").ap()
    gscr = nc.dram_tensor("gscr", [2, D], F32, kind="Internal").ap()

    PE = Eng(K, nc.tensor, "pe")
    ACT = Eng(K, nc.scalar, "act")
    DVE = Eng(K, nc.vector, "dve")
    PO = Eng(K, nc.gpsimd, "pool")
    SP = Eng(K, nc.sync, "sp")
    ENGS = [PE, ACT, DVE, PO, SP]

    def barrier():
        toks = [(e.sem, e.n) for e in ENGS if e.n > 0] + [(d.sem, d.n) for d in K.dsems if d.n > 0]
        for e in ENGS:
            e.wait(toks)

    def epoch():
        barrier()
        for e in ENGS:
            e.new_epoch()

    dumps = {}

    def stop(name):
        if limit != name:
            return
        barrier()
        ds = DmaSem(K, "dump_s")
        for nm, (t, shape, dt) in dumps.items():
            o = nc.dram_tensor("dbg_" + nm, list(shape), dt, kind="ExternalOutput").ap()
            ins = nc.sync.dma_start(out=o, in_=t)
            ds.n += 16
            ins.then_inc(ds.sem, 16)
        nc.sync.wait_ge(ds.sem, ds.n)
        raise _Stop()

    def dma(E, dsem, out, in_, reads=(), writes=(), adds=()):
        E.begin(reads, writes, adds)
        ins = E.eng.dma_start(out=out, in_=in_)
        dsem.n += 16
        ins.then_inc(dsem.sem, 16)
        tok = (dsem.sem, dsem.n)
        for s in reads:
            _tadd(s.r, tok)
        for s in writes:
            s.w = {tok[0]: tok[1]}
            s.r = {}
        for s in adds:
            _tadd(s.w, tok)
        return tok

    def op(E, fn, reads=(), writes=(), adds=()):
        E.begin(reads, writes, adds)
        ins = fn(E.eng)
        return E.end(ins, reads, writes, adds)

    banks = [nc.alloc_psum_tensor("bank%d" % i, [128, 512], F32) for i in range(8)]
    bslot = [Slot() for _ in range(8)]

    def bank_bf(i):
        return banks[i][:].bitcast(BF16)

    K.region(*R_PERSIST)
    ident = K.sb("ident", [128, 128], BF16)
    ones = K.sb("ones", [128, 128], BF16)
    modT = K.sb("modT", [128, 96, 2], F32)
    b_adaT = K.sb("b_adaT", [128, 96], F32)
    norm1T = K.sb("norm1T", [128, KC], F32)
    norm2T = K.sb("norm2T", [128, KC], F32)
    a1 = K.sb("a1", [128, KC], F32)
    a1c = K.sb("a1c", [128, KC], F32)
    a2 = K.sb("a2", [128, KC], F32)
    qnwT = K.sb("qnwT", [128, 3], F32)
    kvnwT = K.sb("kvnwT", [128, 2], F32)
    sublnT = K.sb("sublnT", [128, 1], F32)
    sw = K.sb("sw", [128, 1], F32)
    lamb = K.sb("lamb", [128, 256], F32)
    lwork = K.sb("lwork", [128, 8], F32)
    neglam = K.sb("neglam", [128, 1], F32)
    masks = K.sb("masks", [128, 2], F32)
    ss = K.sb("ss", [128, 64], F32)
    rstd = K.sb("rstd", [128, 64], F32)
    convT = K.sb("convT", [128, FC, 4], F32)
    cvec = K.sb("cvec", [128, KC, 2], F32)
    scT = K.sb("scT", [128, KC, 2], BF16)
    eps_t = K.sb("eps_t", [128, 1], F32)
    s_const = Slot()
    s_modT = Slot()
    s_modT2 = Slot()
    s_der = Slot()
    s_der2 = Slot()
    s_lam = Slot()
    s_ss = [Slot() for _ in range(64)]
    s_rstd = [Slot() for _ in range(64)]

    cs = DmaSem(K, "cs")
    cs2 = DmaSem(K, "cs2")
    for (dst, src) in [(b_adaT[:], b_adaT_d), (norm1T[:], norm1T_d), (norm2T[:], norm2T_d), (qnwT[:], qnwT_d),
                       (kvnwT[:], kvnwT_d), (sublnT[:], sublnT_d), (masks[:], masks_d), (convT[:], convT_d),
                       (cvec[:], cvec_d), (lamb[:], lamb_d.partition_broadcast(128))]:
        dma(SP, cs, dst, src, adds=[s_const])
    dma(PO, cs2, ident[:], ident_d, adds=[s_const])
    op(DVE, lambda e: e.memset(ones[:], 1.0), adds=[s_const])
    op(DVE, lambda e: e.memset(eps_t[:], EPS), adds=[s_const])

    s_lw = Slot()
    op(DVE, lambda e: e.tensor_tensor(out=lamb[:, 0:64], in0=lamb[:, 0:64], in1=lamb[:, 64:128], op=ALU.mult),
       reads=[s_const], writes=[s_lw])
    op(DVE, lambda e: e.tensor_tensor(out=lamb[:, 128:192], in0=lamb[:, 128:192], in1=lamb[:, 192:256], op=ALU.mult),
       reads=[s_const, s_lw], adds=[s_lw])
    op(DVE, lambda e: e.reduce_sum(out=lwork[:, 0:1], in_=lamb[:, 0:64], axis=mybir.AxisListType.X),
       reads=[s_lw], writes=[s_lam])
    op(DVE, lambda e: e.reduce_sum(out=lwork[:, 1:2], in_=lamb[:, 128:192], axis=mybir.AxisListType.X),
       reads=[s_lw], adds=[s_lam])
    s_lam2 = Slot()
    op(ACT, lambda e: e.activation(out=lwork[:, 2:4], in_=lwork[:, 0:2], func=AF.Exp), reads=[s_lam], writes=[s_lam2])
    s_lam3 = Slot()
    op(DVE, lambda e: e.scalar_tensor_tensor(out=neglam[:], in0=lwork[:, 3:4], scalar=-LAMBDA_INIT, in1=lwork[:, 2:3],
                                             op0=ALU.add, op1=ALU.subtract), reads=[s_lam2], writes=[s_lam3])
    op(DVE, lambda e: e.tensor_scalar(out=sw[:], in0=sublnT[:], scalar1=(1.0 - LAMBDA_INIT), scalar2=None, op0=ALU.mult),
       reads=[s_const], adds=[s_lam3])

    dumps["lw"] = (lwork[:], [128, 8], F32)
    dumps["neglam"] = (neglam[:], [128, 1], F32)
    stop("const")
    op(ACT, lambda e: e.activation(out=scT[:], in_=cvec[:], func=AF.Silu), reads=[s_const], writes=[s_der])
    op(DVE, lambda e: e.memset(banks[7][:, 0:192], 0.0), writes=[bslot[7]])
    gs = DmaSem(K, "gs")
    s_gscr = Slot()
    K.region(*R_OT)
    wstA = [K.sb("wadaA%d" % i, [128, 2048], BF16) for i in range(8)]
    K.region(A_ATT + 38976, 4096)
    wstB = [K.sb("wadaB%d" % i, [128, 512], BF16) for i in range(4)]
    WSET = {"A": (wstA, [Slot() for _ in range(8)], [DmaSem(K, "wadaA_s%d" % i) for i in range(8)]),
            "B": (wstB, [Slot() for _ in range(4)], [DmaSem(K, "wadaB_s%d" % i) for i in range(4)])}
    mod_list = []
    for grp in range(6):
        for k in range(KC):
            if grp < 2:
                mod_list.append((grp, k, 0, 2048, "A"))
            else:
                for pc in range(4):
                    mod_list.append((grp, k, pc * 512, 512, "B"))
    mstate = {"dma": 0, "mm": 0, "A_dma": 0, "A_mm": 0, "B_dma": 0, "B_mm": 0, "firstA": True, "firstB": True}

    def mod_dma_ok():
        i = mstate["dma"]
        if i >= len(mod_list):
            return False
        st = mod_list[i][4]
        return mstate[st + "_dma"] - len(WSET[st][0]) < mstate[st + "_mm"]

    def mod_emit_dma():
        i = mstate["dma"]
        grp, k, c0, w, st = mod_list[i]
        bufs, slots, dsems = WSET[st]
        b = mstate[st + "_dma"] % len(bufs)
        dma(PO, dsems[b], bufs[b][:, 0:w], w_ada[k * 128:(k + 1) * 128, grp * D + c0:grp * D + c0 + w],
            writes=[slots[b]])
        mstate[st + "_dma"] += 1
        mstate["dma"] += 1

    def mod_emit_mm():
        i = mstate["mm"]
        grp, k, c0, w, st = mod_list[i]
        bufs, slots, dsems = WSET[st]
        b = mstate[st + "_mm"] % len(bufs)
        mb = 7 if st == "A" else 6
        PE.begin(reads=[slots[b], s_der], adds=[bslot[mb]], xadds=[bslot[mb]])
        if mstate["first" + st]:
            PE.wait(list(bslot[mb].w.items()))
            mstate["first" + st] = False
        for jj in range(w // 128):
            col = (grp * 16 + c0 // 128 + jj) * 2 if st == "A" else 256 + ((grp - 2) * 16 + c0 // 128 + jj) * 2
            ins = nc.tensor.matmul(banks[mb][:, col:col + 2], lhsT=bufs[b][:, jj * 128:(jj + 1) * 128],
                                   rhs=scT[:, k, :], start=False, stop=(k == KC - 1), skip_group_check=True)
        PE.end(ins, reads=[slots[b], s_der], adds=[bslot[mb]])
        mstate[st + "_mm"] += 1
        mstate["mm"] += 1
        last_of_group = (k == KC - 1) and (c0 + w == 2048)
        if last_of_group:
            for c in range(2):
                mb = 7 if grp < 2 else 6
                pc0 = grp * 32 if grp < 2 else 256 + (grp - 2) * 32
                pv = banks[mb][:, pc0:pc0 + 32].rearrange("p (j c) -> p j c", c=2)[:, :, c]
                op(DVE, lambda e, pv=pv, c=c, grp=grp: e.tensor_tensor(out=modT[:, grp * 16:(grp + 1) * 16, c], in0=pv,
                                                                          in1=b_adaT[:, grp * 16:(grp + 1) * 16], op=ALU.add),
                   reads=[s_const], writes=[bslot[mb]], adds=[s_modT if grp < 2 else s_modT2])
            if grp == 1:
                for (dst, cc) in ((a1, 0), (a1c, 1)):
                    op(DVE, lambda e, dst=dst, cc=cc: e.tensor_scalar(out=dst[:], in0=modT[:, 16:32, cc], scalar1=1.0,
                                                                      scalar2=None, op0=ALU.add),
                       reads=[s_modT, s_const], adds=[s_der])
                    op(DVE, lambda e, dst=dst: e.tensor_tensor(out=dst[:], in0=dst[:], in1=norm1T[:], op=ALU.mult),
                       reads=[s_modT, s_const, s_der], adds=[s_der])
            if grp == 5:
                op(DVE, lambda e: e.tensor_scalar(out=a2[:], in0=modT[:, 64:80, 0], scalar1=1.0, scalar2=None,
                                                  op0=ALU.add), reads=[s_modT2, s_const], adds=[s_der2])
                op(DVE, lambda e: e.tensor_tensor(out=a2[:], in0=a2[:], in1=norm2T[:], op=ALU.mult),
                   reads=[s_modT2, s_const, s_der2], adds=[s_der2])
                with nc.allow_non_contiguous_dma(reason="tiny feature-major -> row scatter"):
                    dma(SP, gs, gscr[0].rearrange("(c p) -> p c", p=128), modT[:, 32:48, 0], reads=[s_modT2],
                        adds=[s_gscr])
                    dma(SP, gs, gscr[1].rearrange("(c p) -> p c", p=128), modT[:, 80:96, 0], reads=[s_modT2],
                        adds=[s_gscr])

    def pump_dma_only():
        while mod_dma_ok():
            mod_emit_dma()

    def pump(n, upto=None):
        for _ in range(n):
            lim = len(mod_list) if upto is None else upto
            if mstate["mm"] >= lim:
                return
            while mod_dma_ok() and mstate["dma"] < lim:
                mod_emit_dma()
            mod_emit_mm()
            while mod_dma_ok() and mstate["dma"] < lim:
                mod_emit_dma()

    pump(2, upto=32)
    dumps["modT"] = (modT[:], [128, 96, 2], F32)
    dumps["a1"] = (a1[:], [128, KC], F32)
    stop("mod")

    K.region(*R_HT)
    hT = K.sb("hT", [128, KC, NH], BF16)
    K.region(*R_OT)
    oT = K.sb("oT", [128, 16, NQ], BF16)
    K.region(A_LAT, 20032 + 8320)
    cqT = K.sb("cqT", [128, 3, NQ], BF16)
    ckvT = K.sb("ckvT", [128, 2, NK], BF16)
    kropeT = K.sb("kropeT", [128, NK], BF16)
    cosT = K.sb("cosT", [128, 2050], BF16)
    sinT = K.sb("sinT", [128, 2050], BF16)
    s_hT = Slot()
    s_oT = Slot()
    s_cq = Slot()
    s_ckv = Slot()
    s_krope = Slot()
    s_tab = Slot()
    tabs = DmaSem(K, "tabs")
    dma(PO, tabs, cosT[:], cosT_d, adds=[s_tab])
    dma(PO, tabs, sinT[:], sinT_d, adds=[s_tab])

    K.region(A_ROPE, 4096 + 24576)
    ropet = [K.sb("ropet%d" % i, [128, 512], F32) for i in range(2)]
    ropet_s = [Slot() for _ in range(2)]

    def rope_evac(bk, np_, width, tcol, outs):
        src = banks[bk][:np_, 0:width]
        t1, t2 = ropet[0][:np_, 0:width], ropet[1][:np_, 0:width]
        op(DVE, lambda e: e.tensor_tensor(out=t1, in0=src, in1=cosT[:np_, tcol:tcol + width], op=ALU.mult),
           reads=[bslot[bk], s_tab], writes=[ropet_s[0]])
        first = True
        for g in range(np_ // 32):
            pin = slice(32 * g, 32 * g + 32)
            gout = g + 1 if g % 2 == 0 else g - 1
            pout = slice(32 * gout, 32 * gout + 32)
            fn = lambda e, pin=pin, pout=pout: e.tensor_tensor(out=ropet[1][pout, 0:width], in0=banks[bk][pin, 0:width],
                                                               in1=sinT[pin, tcol:tcol + width], op=ALU.mult)
            if first:
                op(DVE, fn, reads=[bslot[bk], s_tab], writes=[ropet_s[1]])
                first = False
            else:
                op(DVE, fn, reads=[bslot[bk], s_tab], adds=[ropet_s[1]])
        for (plo, phi, clo, chi, dst, dslot) in outs:
            op(DVE, lambda e, plo=plo, phi=phi, clo=clo, chi=chi, dst=dst: e.tensor_tensor(
                out=dst, in0=ropet[0][plo:phi, clo:chi], in1=ropet[1][plo:phi, clo:chi], op=ALU.add),
               reads=[ropet_s[0], ropet_s[1]], adds=[dslot])

    stg = K.sb("stg", [128, KC, 768], BF16)
    s_stg = Slot()
    stg_d = DmaSem(K, "stg_d")

    def load_cols(dst_c0, src, c0, width, first):
        sv = src.rearrange("(k p) n -> p k n", p=128)
        for q in range(4):
            if first and q == 0:
                dma(PO, stg_d, stg[:, 4 * q:4 * q + 4, dst_c0:dst_c0 + width], sv[:, 4 * q:4 * q + 4, c0:c0 + width],
                    writes=[s_stg])
            else:
                dma(PO, stg_d, stg[:, 4 * q:4 * q + 4, dst_c0:dst_c0 + width], sv[:, 4 * q:4 * q + 4, c0:c0 + width],
                    adds=[s_stg])

    load_cols(0, w_in, 3072, 704, True)

    K.region(A_ATT, 36864)
    xs = [K.sb("xs%d" % i, [128, D], F32) for i in range(3)]
    xnb = [K.sb("xnb%d" % i, [128, D], BF16) for i in range(2)]
    NJ = {"junk": K.sb("junk", [128, D], BF16)}
    xs_s = [Slot() for _ in range(3)]
    xn_s = [Slot() for _ in range(2)]
    xs_d = [DmaSem(K, "xs_d%d" % i) for i in range(3)]
    s_junk = Slot()
    tiles = [(128 * i, 128, 128 * i, False) for i in range(8)] + [(1024, 2, 1024, False)] + \
            [(1026 + 128 * i, 128, 1026 + 128 * i, False) for i in range(8)] + \
            [(2050 + 128 * i, 128, 2050 + 128 * i, True) for i in range(2)]

    def norm_tile(src, src_slot, np_, si, xn, xn_slot, inv_d):
        op(ACT, lambda e: e.activation(out=NJ["junk"][:np_, :], in_=src, func=AF.Square, accum_out=ss[:np_, si:si + 1]),
           reads=[src_slot], writes=[s_junk, s_ss[si]])
        op(ACT, lambda e: e.activation(out=rstd[:np_, si:si + 1], in_=ss[:np_, si:si + 1], func=AF.Ln, scale=inv_d,
                                       bias=eps_t[:np_, 0:1]), reads=[s_ss[si], s_const], writes=[s_rstd[si]])
        op(ACT, lambda e: e.activation(out=rstd[:np_, si:si + 1], in_=rstd[:np_, si:si + 1], func=AF.Exp, scale=-0.5),
           writes=[s_rstd[si]])
        op(DVE, lambda e: e.tensor_scalar(out=xn, in0=src, scalar1=rstd[:np_, si:si + 1], scalar2=None, op0=ALU.mult),
           reads=[src_slot, s_rstd[si]], writes=[xn_slot])

    def transpose_tile(xn, xn_slot, np_, col0, av, bv, dst, dst_slot, bk0, mslots):
        for half in range(2):
            bk = bk0 + half
            PE.begin(reads=[xn_slot, s_const], writes=[bslot[bk]])
            for kk in range(8):
                k = half * 8 + kk
                ins = nc.tensor.transpose(out=bank_bf(bk)[:, kk * 128:kk * 128 + np_], in_=xn[:, k * 128:(k + 1) * 128],
                                          identity=ident[:np_, :np_])
            PE.end(ins, reads=[xn_slot, s_const], writes=[bslot[bk]])
            for kk in range(8):
                k = half * 8 + kk
                src = bank_bf(bk)[:, kk * 128:kk * 128 + np_]
                o = dst[:, k, col0:col0 + np_]
                _ev = os.environ.get('P1EV', 'act')
                if _ev == 'none':
                    continue
                if half == 0:
                    op(ACT, lambda e, o=o, src=src, k=k: e.activation(out=o, in_=src, func=AF.Identity,
                                                                      scale=av[:, k:k + 1], bias=bv(k)),
                       reads=[bslot[bk]] + mslots, adds=[dst_slot])
                else:
                    op(DVE, lambda e, o=o, src=src, k=k: e.tensor_scalar(out=o, in0=src, scalar1=av[:, k:k + 1],
                                                                        scalar2=bv(k), op0=ALU.mult, op1=ALU.add),
                       reads=[bslot[bk]] + mslots, adds=[dst_slot])

    def p1_load(i):
        r0, np_, col0, isctx = tiles[i]
        dma(SP, xs_d[i % 3], xs[i % 3][:np_, :], xall[r0:r0 + np_, :], writes=[xs_s[i % 3]])

    def p1_norm(i):
        r0, np_, col0, isctx = tiles[i]
        b = i % 2
        if i + 2 < len(tiles):
            p1_load(i + 2)
        norm_tile(xs[i % 3][:np_, :], xs_s[i % 3], np_, i, xnb[b][:np_, :], xn_s[b], 1.0 / D)

    def p1_tr(i):
        r0, np_, col0, isctx = tiles[i]
        b = i % 2
        xn = xnb[b][:np_, :]
        for half in range(2):
            bk = 2 * b + half
            PE.begin(reads=[xn_s[b], s_const], writes=[bslot[bk]])
            for kk in range(8):
                k = half * 8 + kk
                ins = nc.tensor.transpose(out=bank_bf(bk)[:, kk * 128:kk * 128 + np_], in_=xn[:, k * 128:(k + 1) * 128],
                                          identity=ident[:np_, :np_])
            PE.end(ins, reads=[xn_s[b], s_const], writes=[bslot[bk]])
            src = bank_bf(bk)[:, 0:1024].rearrange("p (k c) -> p k c", c=128)[:, :, 0:np_]
            dst = hT[:, 8 * half:8 * half + 8, col0:col0 + np_]
            if half == 0:
                op(ACT, lambda e: e.activation(out=dst, in_=src, func=AF.Copy), reads=[bslot[bk]], adds=[s_hT])
            else:
                op(DVE, lambda e: e.tensor_copy(out=dst, in_=src), reads=[bslot[bk]], adds=[s_hT])

    p1_load(0)
    p1_load(1)
    p1_norm(0)
    for i in range(len(tiles)):
        if i + 1 < len(tiles):
            p1_norm(i + 1)
        p1_tr(i)
        pump(2, upto=32)
    pump(32, upto=32)
    for k in range(KC):
        kw = {"writes": [s_hT]} if k == 0 else {"adds": [s_hT]}
        op(DVE, lambda e, k=k: e.tensor_scalar(out=hT[:, k, 0:2050], in0=hT[:, k, 0:2050], scalar1=a1[:, k:k + 1],
                                                scalar2=modT[:, k, 0:1], op0=ALU.mult, op1=ALU.add),
           reads=[s_der, s_modT], **kw)
        op(DVE, lambda e, k=k: e.tensor_scalar(out=hT[:, k, 2050:NH], in0=hT[:, k, 2050:NH], scalar1=a1c[:, k:k + 1],
                                                scalar2=modT[:, k, 1:2], op0=ALU.mult, op1=ALU.add),
           reads=[s_der, s_modT], adds=[s_hT])
    epoch()
    dumps["hT"] = (hT[:], [128, KC, NH], BF16)
    stop("P1")

    K.region(A_ATT, 37568)
    latf = [K.sb("latf%d" % i, [128, 3, 512], F32) for i in range(2)]
    latsq = [K.sb("latsq%d" % i, [128, 3, 512], BF16) for i in range(2)]
    latrs = [K.sb("latrs%d" % i, [128, 512], F32) for i in range(2)]
    s_latf = [Slot() for _ in range(2)]
    s_latsq = [Slot() for _ in range(2)]
    s_latrs = [Slot() for _ in range(2)]
    BSET = [(0, 1, 2), (4, 5, 6)]

    def lat_stage1(st, nch, col_off_w, cols, width, krope):
        for c in range(nch):
            bk = BSET[st][c]
            PE.begin(reads=[s_stg, s_hT], writes=[bslot[bk]])
            for k in range(KC):
                ins = nc.tensor.matmul(banks[bk][:, 0:width], lhsT=stg[:, k, col_off_w + c * 128:col_off_w + (c + 1) * 128],
                                       rhs=hT[:, k, cols:cols + width], start=(k == 0), stop=(k == KC - 1))
            PE.end(ins, reads=[s_stg, s_hT], writes=[bslot[bk]])
            kw = {"writes": [s_latf[st]]} if c == 0 else {"adds": [s_latf[st]]}
            op(ACT, lambda e, c=c, bk=bk: e.activation(out=latf[st][:, c, 0:width], in_=banks[bk][:, 0:width],
                                                       func=AF.Copy), reads=[bslot[bk]], **kw)
        op(DVE, lambda e: e.tensor_tensor(out=latsq[st][:, 0:nch, 0:width], in0=latf[st][:, 0:nch, 0:width],
                                          in1=latf[st][:, 0:nch, 0:width], op=ALU.mult),
           reads=[s_latf[st]], writes=[s_latsq[st]])
        if krope is not None:
            kc_, tc = krope
            bk = BSET[st][2]
            PE.begin(reads=[s_stg, s_hT], writes=[bslot[bk]])
            for k in range(KC):
                ins = nc.tensor.matmul(banks[bk][0:64, 0:width], lhsT=stg[:, k, 640:704], rhs=hT[:, k, cols:cols + width],
                                       start=(k == 0), stop=(k == KC - 1))
            PE.end(ins, reads=[s_stg, s_hT], writes=[bslot[bk]])
            if tc is None:
                op(ACT, lambda e: e.activation(out=kropeT[0:64, kc_:kc_ + width], in_=banks[bk][0:64, 0:width],
                                               func=AF.Copy), reads=[bslot[bk]], adds=[s_krope])
            else:
                rope_evac(bk, 64, width, tc, [(0, 64, 0, width, kropeT[0:64, kc_:kc_ + width], s_krope)])

    def lat_stage2(st, nch, width, nfeat, wT, dst, dst_slot, dcol):
        PE.begin(reads=[s_latsq[st], s_const], writes=[bslot[3]])
        for c in range(nch):
            ins = nc.tensor.matmul(banks[3][:, 0:width], lhsT=ones[:], rhs=latsq[st][:, c, 0:width], start=(c == 0),
                                   stop=(c == nch - 1))
        PE.end(ins, reads=[s_latsq[st], s_const], writes=[bslot[3]])
        op(ACT, lambda e: e.activation(out=latrs[st][:, 0:width], in_=banks[3][:, 0:width], func=AF.Ln,
                                       scale=1.0 / nfeat, bias=eps_t[:, 0:1]), reads=[bslot[3], s_const],
           writes=[s_latrs[st]])
        op(ACT, lambda e: e.activation(out=latrs[st][:, 0:width], in_=latrs[st][:, 0:width], func=AF.Exp, scale=-0.5),
           writes=[s_latrs[st]])
        for c in range(nch):
            op(DVE, lambda e, c=c: e.scalar_tensor_tensor(out=dst[:, c, dcol:dcol + width], in0=latf[st][:, c, 0:width],
                                                           scalar=wT[:, c:c + 1], in1=latrs[st][:, 0:width],
                                                           op0=ALU.mult, op1=ALU.mult),
               reads=[s_latf[st], s_latrs[st], s_const], adds=[dst_slot])

    op(DVE, lambda e: e.memset(kropeT[64:128, :], 0.0), adds=[s_krope])
    lblocks = []
    for blk in range(3):
        lblocks.append(((3, 0, blk * 342, 342, None), (3, 342, 384.0, qnwT, cqT, s_cq, blk * 342)))
    for (hc, kc_, w_, tc) in KBLOCKS:
        lblocks.append(((2, 384, hc, w_, (kc_, tc)), (2, w_, 256.0, kvnwT, ckvT, s_ckv, kc_)))
    for i, (a1_, a2_) in enumerate(lblocks):
        lat_stage1(i % 2, *a1_)
        if i > 0:
            lat_stage2((i - 1) % 2, *lblocks[i - 1][1])
    lat_stage2((len(lblocks) - 1) % 2, *lblocks[-1][1])
    epoch()
    dumps["cqT"] = (cqT[:], [128, 3, NQ], BF16)
    dumps["ckvT"] = (ckvT[:], [128, 2, NK], BF16)
    dumps["kropeT"] = (kropeT[0:64, :], [64, NK], BF16)
    stop("P2")

    K.region(A_ATT, 38976)
    KT = K.sb("KT", [128, 2, NK], BF16)
    QB = K.sb("QB", [128, 2, 6, 342], BF16)
    QR = K.sb("QR", [128, 2, NQ], BF16)
    VV = K.sb("VV", [128, 18, 256], BF16)
    PT = [K.sb("PT%d" % i, [128, 342], BF16) for i in range(5)]
    fin_r = K.sb("fin_r", [128, 342], F32)
    fin_t = K.sb("fin_t", [128, 342], F32)
    fin_o = K.sb("fin_o", [128, 171], F32)
    fin_sq = K.sb("fin_sq", [128, 171], BF16)
    fin_rs = K.sb("fin_rs", [128, 171], F32)
    s_KT, s_QB, s_QR, s_VV = Slot(), Slot(), Slot(), Slot()
    PT_s = [Slot() for _ in range(5)]
    s_fr, s_ft, s_fo, s_fsq, s_frs = Slot(), Slot(), Slot(), Slot(), Slot()
    op(DVE, lambda e: e.memset(QB[:], 0.0), writes=[s_QB])
    op(DVE, lambda e: e.memset(QR[64:128, :, :], 0.0), writes=[s_QR])

    pt_ctr = [0]
    sb_ctr = [0]

    NPT = 5
    DEPTH = 2
    NSB = 3
    OVB = [4, 7]
    SMBS = [5, 3]
    DEFER = 12
    PUMP_EVERY = [0]
    PUMP_HOLD = [0]

    def attention(nblk, bw, s_mm, v_of, exp_scale, fin_a, fin_b):
        steps = [(j, t) for j in range(nblk) for t in range(18)]
        sbank = {}

        def emit_s(i):
            j, t = steps[i]
            sbk = sb_ctr[0] % NSB
            sb_ctr[0] += 1
            sbank[i] = sbk
            s_mm(j, t, sbk)

        for i in range(min(DEPTH, len(steps))):
            emit_s(i)
        pending = None
        for i, (j, t) in enumerate(steps):
            sbk = sbank[i]
            p = pt_ctr[0] % NPT
            pt_ctr[0] += 1
            op(ACT, lambda e, p=p, sbk=sbk: e.activation(out=PT[p][:, 0:bw], in_=banks[sbk][:, 0:bw], func=AF.Exp,
                                                          scale=exp_scale), reads=[bslot[sbk]], writes=[PT_s[p]])
            if i + DEPTH < len(steps):
                emit_s(i + DEPTH)
            ovb = OVB[j % len(OVB)]
            smb = SMBS[j % 2]
            pv_mm(j, t, p, ovb, smb, bw, v_of)
            if PUMP_HOLD[0] > 0:
                PUMP_HOLD[0] -= 1
            elif PUMP_EVERY[0] and i % PUMP_EVERY[0] == 0:
                pump(1)
            if t == DEFER and pending is not None:
                fin_b(pending, 1)
            if t == DEFER + 3 and pending is not None:
                fin_b(pending, 2)
                pending = None
            if t == 17:
                fin_a(j, ovb, smb)
                pending = j
        if pending is not None:
            fin_b(pending, 1)
            fin_b(pending, 2)

    def pv_mm(j, t, p, ovb, smb, bw, v_of):
        first = (t == 0)
        last = (t == 17)
        if first:
            PE.begin(reads=[PT_s[p], s_VV, s_const], writes=[bslot[ovb], bslot[smb]])
        else:
            PE.begin(reads=[PT_s[p], s_VV, s_const])
        nc.tensor.matmul(banks[ovb][:, 0:bw], lhsT=v_of(t), rhs=PT[p][:, 0:bw], start=first, stop=last)
        ins = nc.tensor.matmul(banks[smb][:, 0:bw], lhsT=ones[:], rhs=PT[p][:, 0:bw], start=first, stop=last)
        if first:
            PE.end(ins, reads=[PT_s[p], s_VV, s_const], writes=[bslot[ovb], bslot[smb]])
        else:
            PE.end(ins, reads=[PT_s[p], s_VV, s_const], adds=[bslot[ovb], bslot[smb]])

    K.region(A_STG, 24576)
    wuq = K.sb("wuq", [128, 3, 1536], BF16)
    wukv = K.sb("wukv", [128, 2, 2048], BF16)
    K.region(*R_HT)
    x1 = K.sb("x1", [128, 8, D], F32)
    x1h = K.sb("x1h", [128, D], F32)
    s_x1 = [Slot() for _ in range(9)]
    xl = DmaSem(K, "xl")

    def x1t(i):
        return (x1[:, i, :], 128) if i < 8 else (x1h[0:2, :], 2)

    s_wu = Slot()
    wu_d = DmaSem(K, "wu_d")

    def prefetch_mla():
        for i in range(9):
            ap_, np_ = x1t(i)
            dma(SP, xl, ap_, xall[128 * i:128 * i + np_, :], writes=[s_x1[i]], adds=[s_hT])
        for i in range(9):
            s_x1[i].w = {xl.sem: xl.n}
        dma(PO, wu_d, wuq[:], w_uq.rearrange("(k p) n -> p k n", p=128), writes=[s_stg], adds=[s_wu])
        dma(PO, wu_d, wukv[:], w_ukv.rearrange("(k p) n -> p k n", p=128), adds=[s_wu, s_stg])

    def load_pair(pr):
        load_cols(0, w_in, pr * 256, 256, True)
        load_cols(256, w_in, 1024 + pr * 256, 256, False)
        load_cols(512, w_in, 2048 + pr * 256, 256, False)

    load_pair(0)
    op(DVE, lambda e: e.memset(banks[6][:, 256:448], 0.0), writes=[bslot[6]])
    pump_dma_only()
    PUMP_EVERY[0] = 4
    for pr in range(4):
        rb = [0]

        def nextbank():
            b = rb[0] % 4
            rb[0] += 1
            return b
        def v_tile(t):
            bk = nextbank()
            c0 = ktile_col(t)
            PE.begin(reads=[s_stg, s_hT], writes=[bslot[bk]])
            for k in range(KC):
                ins = nc.tensor.matmul(banks[bk][:, 0:256], lhsT=hT[:, k, c0:c0 + 128], rhs=stg[:, k, 512:768],
                                       start=(k == 0), stop=(k == KC - 1))
            PE.end(ins, reads=[s_stg, s_hT], writes=[bslot[bk]])
            if t == 0:
                ACT.begin(writes=[s_VV])
            op(ACT, lambda e: e.activation(out=VV[:, t, :], in_=banks[bk][:, 0:256], func=AF.Copy),
               reads=[bslot[bk]], adds=[s_VV])

        vq = [(lambda t=t: v_tile(t)) for t in range(18)]

        def v_some(n):
            for _ in range(n):
                if vq:
                    vq.pop(0)()

        for hh in range(2):
            for blk in range(3):
                bk = nextbank()
                PE.begin(reads=[s_stg, s_hT], writes=[bslot[bk]])
                for k in range(KC):
                    ins = nc.tensor.matmul(banks[bk][:, 0:342], lhsT=stg[:, k, hh * 128:(hh + 1) * 128],
                                           rhs=hT[:, k, blk * 342:(blk + 1) * 342], start=(k == 0), stop=(k == KC - 1))
                PE.end(ins, reads=[s_stg, s_hT], writes=[bslot[bk]])
                outs = []
                for sub in range(2):
                    j = 2 * blk + sub
                    outs.append((0, 64, sub * 171, (sub + 1) * 171, QB[0:64, hh, j, 0:171], s_QB))
                    outs.append((64, 128, sub * 171, (sub + 1) * 171, QB[64:128, hh, j, 171:342], s_QB))
                if hh == 0 and blk == 0:
                    DVE.begin(writes=[s_QB])
                rope_evac(bk, 128, 342, blk * 342, outs)
                v_some(1)
        for hh in range(2):
            for (hc, kc_, w_, tc) in KBLOCKS:
                bk = nextbank()
                PE.begin(reads=[s_stg, s_hT], writes=[bslot[bk]])
                for k in range(KC):
                    ins = nc.tensor.matmul(banks[bk][:, 0:w_], lhsT=stg[:, k, 256 + hh * 128:256 + (hh + 1) * 128],
                                           rhs=hT[:, k, hc:hc + w_], start=(k == 0), stop=(k == KC - 1))
                PE.end(ins, reads=[s_stg, s_hT], writes=[bslot[bk]])
                if hh == 0 and kc_ == 0:
                    DVE.begin(writes=[s_KT])
                    ACT.begin(writes=[s_KT])
                if tc is None:
                    op(ACT, lambda e, hh=hh, kc_=kc_, w_=w_, bk=bk: e.activation(out=KT[:, hh, kc_:kc_ + w_],
                                                                                 in_=banks[bk][:, 0:w_], func=AF.Copy),
                       reads=[bslot[bk]], adds=[s_KT])
                else:
                    rope_evac(bk, 128, w_, tc, [(0, 128, 0, w_, KT[:, hh, kc_:kc_ + w_], s_KT)])
                v_some(1)
        while vq:
            vq.pop(0)()
        if pr < 3:
            load_pair(pr + 1)
        else:
            prefetch_mla()
        PUMP_HOLD[0] = 40
        for hh in range(2):
            head = 2 * pr + hh

            def s_mm(j, t, sbk, hh=hh):
                PE.begin(reads=[s_KT, s_QB], writes=[bslot[sbk]])
                ins = nc.tensor.matmul(banks[sbk][:, 0:342], lhsT=KT[:, hh, t * 128:(t + 1) * 128], rhs=QB[:, hh, j, :],
                                       start=True, stop=True)
                PE.end(ins, reads=[s_KT, s_QB], writes=[bslot[sbk]])

            def v_of(t, hh=hh):
                return VV[:, t, hh * 128:(hh + 1) * 128]

            def fin_a(j, ovb, smb, head=head):
                op(DVE, lambda e: e.reciprocal(out=fin_r[:], in_=banks[smb][:, 0:342]), reads=[bslot[smb]], writes=[s_fr])
                op(DVE, lambda e: e.tensor_tensor(out=fin_t[:], in0=banks[ovb][:, 0:342], in1=fin_r[:], op=ALU.mult),
                   reads=[bslot[ovb], s_fr], writes=[s_ft])
                op(DVE, lambda e: e.scalar_tensor_tensor(out=fin_o[:], in0=fin_t[:, 171:342], scalar=neglam[:, 0:1],
                                                         in1=fin_t[:, 0:171], op0=ALU.mult, op1=ALU.add),
                   reads=[s_ft, s_lam3], writes=[s_fo])
                op(DVE, lambda e: e.tensor_tensor(out=fin_sq[:], in0=fin_o[:], in1=fin_o[:], op=ALU.mult),
                   reads=[s_fo], writes=[s_fsq])

            def fin_b(j, stage, head=head):
                if stage == 1:
                    op(DVE, lambda e: e.memset(banks[6][:, 0:171], 0.0), writes=[bslot[6]])
                    PE.begin(reads=[s_fsq, s_const], adds=[bslot[6]], xadds=[bslot[6]])
                    ins = nc.tensor.matmul(banks[6][:, 0:171], lhsT=ones[:], rhs=fin_sq[:], start=False, stop=True,
                                           skip_group_check=True)
                    PE.end(ins, reads=[s_fsq, s_const], adds=[bslot[6]])
                    return
                op(ACT, lambda e: e.activation(out=fin_rs[:], in_=banks[6][:, 0:171], func=AF.Ln, scale=1.0 / 128.0,
                                               bias=eps_t[:, 0:1]), reads=[bslot[6]], writes=[s_frs])
                op(ACT, lambda e: e.activation(out=fin_rs[:], in_=fin_rs[:], func=AF.Exp, scale=-0.5), writes=[s_frs])
                op(DVE, lambda e: e.scalar_tensor_tensor(out=oT[:, head, j * 171:(j + 1) * 171], in0=fin_o[:],
                                                         scalar=sw[:, 0:1], in1=fin_rs[:], op0=ALU.mult, op1=ALU.mult),
                   reads=[s_fo, s_frs, s_lam3], adds=[s_oT])

            attention(6, 342, s_mm, v_of, DA_SCALE, fin_a, fin_b)
        if pr == 0:
            dumps["KT"] = (KT[:], [128, 2, NK], BF16)
            dumps["QB"] = (QB[:], [128, 2, 6, 342], BF16)
            dumps["VV"] = (VV[:], [128, 18, 256], BF16)
            dumps["oT"] = (oT[:], [128, 16, NQ], BF16)
            stop("DA0")
    epoch()
    stop("DA")


    QN = QB[:].rearrange("p h j c -> p h (j c)")
    PUMP_EVERY[0] = 4
    for pr in range(4):
        rb = [0]

        def nextbank():
            b = rb[0] % 4
            rb[0] += 1
            return b
        for hh in range(2):
            h = 2 * pr + hh
            for blk in range(3):
                bk = nextbank()
                PE.begin(reads=[s_wu, s_cq], writes=[bslot[bk]])
                for k in range(3):
                    ins = nc.tensor.matmul(banks[bk][:, 0:342], lhsT=wuq[:, k, h * 192:h * 192 + 128],
                                           rhs=cqT[:, k, blk * 342:(blk + 1) * 342], start=(k == 0), stop=(k == 2))
                PE.end(ins, reads=[s_wu, s_cq], writes=[bslot[bk]])
                if hh == 0 and blk == 0:
                    ACT.begin(writes=[s_QB])
                    DVE.begin(writes=[s_QR])
                op(ACT, lambda e, hh=hh, blk=blk, bk=bk: e.activation(out=QN[:, hh, blk * 342:(blk + 1) * 342],
                                                                      in_=banks[bk][:, 0:342], func=AF.Copy),
                   reads=[bslot[bk]], adds=[s_QB])
                bk = nextbank()
                PE.begin(reads=[s_wu, s_cq], writes=[bslot[bk]])
                for k in range(3):
                    ins = nc.tensor.matmul(banks[bk][0:64, 0:342], lhsT=wuq[:, k, h * 192 + 128:h * 192 + 192],
                                           rhs=cqT[:, k, blk * 342:(blk + 1) * 342], start=(k == 0), stop=(k == 2))
                PE.end(ins, reads=[s_wu, s_cq], writes=[bslot[bk]])
                rope_evac(bk, 64, 342, blk * 342, [(0, 64, 0, 342, QR[0:64, hh, blk * 342:(blk + 1) * 342], s_QR)])
            for (hc, kc_, w_, tc) in KBLOCKS:
                bk = nextbank()
                PE.begin(reads=[s_wu, s_ckv], writes=[bslot[bk]])
                for k in range(2):
                    ins = nc.tensor.matmul(banks[bk][:, 0:w_], lhsT=wukv[:, k, h * 256:h * 256 + 128],
                                           rhs=ckvT[:, k, kc_:kc_ + w_], start=(k == 0), stop=(k == 1))
                PE.end(ins, reads=[s_wu, s_ckv], writes=[bslot[bk]])
                if hh == 0 and kc_ == 0:
                    DVE.begin(writes=[s_KT])
                    ACT.begin(writes=[s_KT])
                if (kc_ // 512) % 2 == 0:
                    op(ACT, lambda e, hh=hh, kc_=kc_, w_=w_, bk=bk: e.activation(out=KT[:, hh, kc_:kc_ + w_],
                                                                                 in_=banks[bk][:, 0:w_], func=AF.Copy),
                       reads=[bslot[bk]], adds=[s_KT])
                else:
                    op(DVE, lambda e, hh=hh, kc_=kc_, w_=w_, bk=bk: e.tensor_copy(out=KT[:, hh, kc_:kc_ + w_],
                                                                                 in_=banks[bk][:, 0:w_]),
                       reads=[bslot[bk]], adds=[s_KT])
        for t in range(18):
            bk = nextbank()
            PE.begin(reads=[s_wu, s_ckv], writes=[bslot[bk]])
            for hh in range(2):
                h = 2 * pr + hh
                for k in range(2):
                    ins = nc.tensor.matmul(banks[bk][:, hh * 128:(hh + 1) * 128], lhsT=ckvT[:, k, t * 128:(t + 1) * 128],
                                           rhs=wukv[:, k, h * 256 + 128:h * 256 + 256], start=(k == 0), stop=(k == 1))
            PE.end(ins, reads=[s_wu, s_ckv], writes=[bslot[bk]])
            if t == 0:
                DVE.begin(writes=[s_VV])
                ACT.begin(writes=[s_VV])
            if t % 2 == 0:
                op(ACT, lambda e, t=t, bk=bk: e.activation(out=VV[:, t, :], in_=banks[bk][:, 0:256], func=AF.Copy),
                   reads=[bslot[bk]], adds=[s_VV])
            else:
                op(DVE, lambda e, t=t, bk=bk: e.tensor_copy(out=VV[:, t, :], in_=banks[bk][:, 0:256]),
                   reads=[bslot[bk]], adds=[s_VV])
        for hh in range(2):
            head = 8 + 2 * pr + hh

            def s_mm(j, t, sbk, hh=hh):
                PE.begin(reads=[s_KT, s_QB, s_QR, s_krope], writes=[bslot[sbk]])
                nc.tensor.matmul(banks[sbk][:, 0:342], lhsT=KT[:, hh, t * 128:(t + 1) * 128],
                                 rhs=QN[:, hh, j * 342:(j + 1) * 342], start=True, stop=False)
                ins = nc.tensor.matmul(banks[sbk][:, 0:342], lhsT=kropeT[:, t * 128:(t + 1) * 128],
                                       rhs=QR[:, hh, j * 342:(j + 1) * 342], start=False, stop=True)
                PE.end(ins, reads=[s_KT, s_QB, s_QR, s_krope], writes=[bslot[sbk]])

            def v_of(t, hh=hh):
                return VV[:, t, hh * 128:(hh + 1) * 128]

            def fin_a(j, ovb, smb, head=head):
                op(DVE, lambda e: e.reciprocal(out=fin_r[:], in_=banks[smb][:, 0:342]), reads=[bslot[smb]], writes=[s_fr])
                op(DVE, lambda e: e.tensor_tensor(out=oT[:, head, j * 342:(j + 1) * 342], in0=banks[ovb][:, 0:342],
                                                  in1=fin_r[:], op=ALU.mult), reads=[bslot[ovb], s_fr], adds=[s_oT])

            attention(3, 342, s_mm, v_of, MLA_SCALE, fin_a, lambda j, stage: None)
    pump(100000)
    PUMP_EVERY[0] = 0
    epoch()
    stop("MLA")

    K.region(*R_REST)
    h2T = K.sb("h2T", [128, KC, NQ], BF16)
    mC = K.cur[1]
    wo = [K.sb("wo%d" % i, [128, KC, 512], BF16) for i in range(2)]
    g1_b = K.sb("g1_b", [128, D], F32)
    tmpB = [K.sb("tmpB%d" % i, [128, 512], F32) for i in range(2)]
    xn2 = [K.sb("xn2_%d" % i, [128, D], BF16) for i in range(2)]
    NJ["junk"] = K.sb("junkB", [128, D], BF16)
    s_h2T = Slot()
    wo_s = [Slot() for _ in range(2)]
    wo_d = [DmaSem(K, "wo_d%d" % i) for i in range(2)]
    s_g1b = Slot()
    tmpB_s = [Slot() for _ in range(2)]
    xn2_s = [Slot() for _ in range(2)]
    dma(SP, DmaSem(K, "g1s"), g1_b[:], gscr[0].partition_broadcast(128), reads=[s_gscr], writes=[s_g1b])
    wov = w_o.rearrange("(k p) n -> p k n", p=128)
    ev = [0]
    def load_wo(n):
        b = n % 2
        for q in range(4):
            if q == 0:
                dma(PO, wo_d[b], wo[b][:, 0:4, :], wov[:, 0:4, n * 512:(n + 1) * 512], writes=[wo_s[b]])
            else:
                dma(PO, wo_d[b], wo[b][:, 4 * q:4 * q + 4, :], wov[:, 4 * q:4 * q + 4, n * 512:(n + 1) * 512],
                    adds=[wo_s[b]])

    def b2_norm(i):
        ap_, np_ = x1t(i)
        b = i % 2
        norm_tile(ap_, s_x1[i], np_, 20 + i, xn2[b][:np_, :], xn2_s[b], 1.0 / D)

    def b2_tr(i):
        ap_, np_ = x1t(i)
        b = i % 2
        transpose_tile(xn2[b][:np_, :], xn2_s[b], np_, 128 * i, a2, (lambda k: modT[:, 48 + k, 0:1]), h2T, s_h2T,
                       4 + 2 * b, [s_der2, s_modT2])

    load_wo(0)
    for n in range(4):
        b = n % 2
        if n < 3:
            load_wo(n + 1)
        for i in range(9):
            ap_, np_ = x1t(i)
            c0 = 128 * i
            bk = ev[0] % 4
            tb = ev[0] % 2
            ev[0] += 1
            PE.begin(reads=[wo_s[b], s_oT], writes=[bslot[bk]])
            for k in range(KC):
                ins = nc.tensor.matmul(banks[bk][:np_, :], lhsT=oT[:, k, c0:c0 + np_], rhs=wo[b][:, k, :],
                                       start=(k == 0), stop=(k == KC - 1))
            PE.end(ins, reads=[wo_s[b], s_oT], writes=[bslot[bk]])
            op(DVE, lambda e, bk=bk, tb=tb, np_=np_, n=n: e.tensor_tensor(out=tmpB[tb][:np_, :], in0=banks[bk][:np_, :],
                                                                           in1=g1_b[:np_, n * 512:(n + 1) * 512],
                                                                           op=ALU.mult),
               reads=[bslot[bk], s_g1b], writes=[tmpB_s[tb]])
            xa = ap_[:, n * 512:(n + 1) * 512]
            op(DVE, lambda e, xa=xa, tb=tb, np_=np_: e.tensor_tensor(out=xa, in0=xa, in1=tmpB[tb][:np_, :], op=ALU.add),
               reads=[tmpB_s[tb]], writes=[s_x1[i]])
            if n == 3:
                b2_norm(i)
                if i >= 1:
                    b2_tr(i - 1)
    b2_tr(8)
    op(DVE, lambda e: e.tensor_scalar(out=h2T[:, :, 1024:1025], in0=h2T[:, :, 1024:1025], scalar1=masks[:, 0:1],
                                      scalar2=None, op0=ALU.mult), reads=[s_const], writes=[s_h2T])
    op(DVE, lambda e: e.tensor_scalar(out=h2T[:, :, 1025:1026], in0=h2T[:, :, 1025:1026], scalar1=masks[:, 1:2],
                                      scalar2=None, op0=ALU.mult), reads=[s_const], writes=[s_h2T])
    epoch()
    dumps["h2T"] = (h2T[:], [128, KC, NQ], BF16)
    dumps["x1"] = (x1[:], [128, 8, D], F32)
    stop("B")

    K.region(*R_OT)
    aT = K.sb("aT", [128, SEG, 1024], BF16)
    g2_b = K.sb("g2_b", [128, D], F32)
    K.region(mC, R_REST[0] + R_REST[1] - mC)
    mF = K.cur[1]
    wup = [K.sb("wup%d" % i, [128, KC, 256], BF16) for i in range(3)]
    wdn = [K.sb("wdn%d" % i, [128, SEG, 512], BF16) for i in range(2)]
    G = K.sb("G", [128, NQ], F32)
    Tc = K.sb("Tc", [128, 1024], F32)
    Sg = K.sb("Sg", [128, 1024], F32)
    tmpC = [K.sb("tmpC%d" % i, [128, 512], F32) for i in range(2)]
    wup_s = [Slot() for _ in range(3)]
    wup_d = [DmaSem(K, "wup_d%d" % i) for i in range(3)]
    wdn_s = [Slot() for _ in range(2)]
    wdn_d = [DmaSem(K, "wdn_d%d" % i) for i in range(2)]
    s_aT, s_G, s_Tc, s_Sg, s_g2b = Slot(), Slot(), Slot(), Slot(), Slot()
    tmpC_s = [Slot() for _ in range(2)]
    dma(SP, DmaSem(K, "g2s"), g2_b[:], gscr[1].partition_broadcast(128), reads=[s_gscr], writes=[s_g2b])
    wupv = w_up.rearrange("(k p) n -> p k n", p=128)
    wdnv = w_down.rearrange("(c p) n -> p c n", p=128)
    rbk = [0]
    UB = [(0, 342), (342, 342), (684, 340)]
    wdn_it = [0]
    def load_wup(cg):
        wb = cg % 3
        for q in range(2):
            dma(PO, wup_d[wb], wup[wb][:, 8 * q:8 * q + 8, 0:128], wupv[:, 8 * q:8 * q + 8, cg * 128:(cg + 1) * 128],
                **({"writes": [wup_s[wb]]} if q == 0 else {"adds": [wup_s[wb]]}))
        for q in range(2):
            dma(PO, wup_d[wb], wup[wb][:, 8 * q:8 * q + 8, 128:256],
                wupv[:, 8 * q:8 * q + 8, DFF + cg * 128:DFF + (cg + 1) * 128], adds=[wup_s[wb]])

    def load_wdn(sg, n):
        db = (sg * 4 + n) % 2
        r0 = sg * SEG
        dma(PO, wdn_d[db], wdn[db][:, 0:6, :], wdnv[:, r0:r0 + 6, n * 512:(n + 1) * 512], writes=[wdn_s[db]])
        dma(PO, wdn_d[db], wdn[db][:, 6:SEG, :], wdnv[:, r0 + 6:r0 + SEG, n * 512:(n + 1) * 512], adds=[wdn_s[db]])

    K.region(R_REST[0], 32832)
    fw_b = K.sb("fw_b", [128, D], F32)
    yt = [K.sb("yt%d" % i, [128, D], F32) for i in range(2)]
    junkF = K.sb("junkF", [128, D], BF16)
    s_fw = Slot()
    yt_s = [Slot() for _ in range(2)]
    ys = [DmaSem(K, "ys%d" % i) for i in range(2)]

    def final_tile(i):
        b = i % 2
        si = 40 + i
        op(ACT, lambda e: e.activation(out=junkF[:], in_=x1[:, i, :], func=AF.Square, accum_out=ss[:, si:si + 1]),
           reads=[s_x1[i]], writes=[s_junk, s_ss[si]])
        op(ACT, lambda e: e.activation(out=rstd[:, si:si + 1], in_=ss[:, si:si + 1], func=AF.Ln, scale=1.0 / D,
                                       bias=eps_t[:, 0:1]), reads=[s_ss[si]], writes=[s_rstd[si]])
        op(ACT, lambda e: e.activation(out=rstd[:, si:si + 1], in_=rstd[:, si:si + 1], func=AF.Exp, scale=-0.5),
           writes=[s_rstd[si]])
        op(DVE, lambda e: e.scalar_tensor_tensor(out=yt[b][:], in0=x1[:, i, :], scalar=rstd[:, si:si + 1],
                                                 in1=fw_b[:], op0=ALU.mult, op1=ALU.mult),
           reads=[s_x1[i], s_rstd[si], s_fw], writes=[yt_s[b]])
        dma(SP, ys[b], y[128 * i:128 * (i + 1), :], yt[b][:], reads=[yt_s[b]])

    load_wup(0)
    load_wup(1)
    for sg in range(NSEG):
        for c in range(SEG):
            cg = sg * SEG + c
            wb = cg % 3
            if cg + 2 < FC:
                load_wup(cg + 2)
            if c == SEG - 2:
                load_wdn(sg, 0)
            gb = []
            for blk in range(3):
                bk = rbk[0] % 8
                rbk[0] += 1
                gb.append(bk)
                PE.begin(reads=[wup_s[wb], s_h2T], writes=[bslot[bk]])
                for k in range(KC):
                    ins = nc.tensor.matmul(banks[bk][:, 0:342], lhsT=wup[wb][:, k, 0:128],
                                           rhs=h2T[:, k, blk * 342:(blk + 1) * 342], start=(k == 0), stop=(k == KC - 1))
                PE.end(ins, reads=[wup_s[wb], s_h2T], writes=[bslot[bk]])
                if blk == 0:
                    op(ACT, lambda e, bk=bk, blk=blk: e.activation(out=G[:, blk * 342:(blk + 1) * 342],
                                                                   in_=banks[bk][:, 0:342], func=AF.Copy),
                       reads=[bslot[bk]], writes=[s_G])
                else:
                    op(ACT, lambda e, bk=bk, blk=blk: e.activation(out=G[:, blk * 342:(blk + 1) * 342],
                                                                   in_=banks[bk][:, 0:342], func=AF.Copy),
                       reads=[bslot[bk]], adds=[s_G])
            ub = []
            for blk in range(3):
                bk = rbk[0] % 8
                rbk[0] += 1
                ub.append(bk)
                u0, uw = UB[blk]
                PE.begin(reads=[wup_s[wb], s_h2T], writes=[bslot[bk]])
                for k in range(KC):
                    ins = nc.tensor.matmul(banks[bk][:, 0:uw], lhsT=wup[wb][:, k, 128:256], rhs=h2T[:, k, u0:u0 + uw],
                                           start=(k == 0), stop=(k == KC - 1))
                PE.end(ins, reads=[wup_s[wb], s_h2T], writes=[bslot[bk]])
            w0, w1, w2, cb = (convT[:, cg, j:j + 1] for j in range(4))
            op(DVE, lambda e: e.tensor_scalar(out=Tc[:, 0:1024], in0=G[:, 0:1024], scalar1=w1, scalar2=cb, op0=ALU.mult,
                                              op1=ALU.add), reads=[s_G, s_const], writes=[s_Tc])
            op(DVE, lambda e: e.scalar_tensor_tensor(out=Tc[:, 1:1024], in0=G[:, 0:1023], scalar=w0, in1=Tc[:, 1:1024],
                                                     op0=ALU.mult, op1=ALU.add), reads=[s_G], writes=[s_Tc])
            op(DVE, lambda e: e.scalar_tensor_tensor(out=Tc[:, 0:1023], in0=G[:, 1:1024], scalar=w2, in1=Tc[:, 0:1023],
                                                     op0=ALU.mult, op1=ALU.add), reads=[s_G], writes=[s_Tc])
            op(DVE, lambda e: e.scalar_tensor_tensor(out=Tc[:, 0:1], in0=G[:, 1024:1025], scalar=w0, in1=Tc[:, 0:1],
                                                     op0=ALU.mult, op1=ALU.add), reads=[s_G], writes=[s_Tc])
            op(DVE, lambda e: e.scalar_tensor_tensor(out=Tc[:, 1023:1024], in0=G[:, 1025:1026], scalar=w2,
                                                     in1=Tc[:, 1023:1024], op0=ALU.mult, op1=ALU.add),
               reads=[s_G], writes=[s_Tc])
            op(ACT, lambda e: e.activation(out=Sg[:], in_=Tc[:], func=AF.Silu), reads=[s_Tc], writes=[s_Sg])
            for blk in range(3):
                u0, uw = UB[blk]
                bk = ub[blk]
                fn = lambda e, bk=bk, u0=u0, uw=uw, c=c: e.tensor_tensor(out=aT[:, c, u0:u0 + uw], in0=banks[bk][:, 0:uw],
                                                                         in1=Sg[:, u0:u0 + uw], op=ALU.mult)
                if c == 0 and blk == 0:
                    op(DVE, fn, reads=[bslot[bk], s_Sg], writes=[s_aT])
                else:
                    op(DVE, fn, reads=[bslot[bk], s_Sg], adds=[s_aT])
        for n in range(4):
            db = (sg * 4 + n) % 2
            if n < 3:
                load_wdn(sg, n + 1)
            for i in range(8):
                bk = rbk[0] % 8
                rbk[0] += 1
                tb = i % 2
                PE.begin(reads=[wdn_s[db], s_aT], writes=[bslot[bk]])
                for c in range(SEG):
                    ins = nc.tensor.matmul(banks[bk][:, :], lhsT=aT[:, c, i * 128:(i + 1) * 128], rhs=wdn[db][:, c, :],
                                           start=(c == 0), stop=(c == SEG - 1))
                PE.end(ins, reads=[wdn_s[db], s_aT], writes=[bslot[bk]])
                op(DVE, lambda e, bk=bk, tb=tb, n=n: e.tensor_tensor(out=tmpC[tb][:], in0=banks[bk][:, :],
                                                                      in1=g2_b[:, n * 512:(n + 1) * 512], op=ALU.mult),
                   reads=[bslot[bk], s_g2b], writes=[tmpC_s[tb]])
                xa = x1[:, i, n * 512:(n + 1) * 512]
                op(DVE, lambda e, xa=xa, tb=tb: e.tensor_tensor(out=xa, in0=xa, in1=tmpC[tb][:], op=ALU.add),
                   reads=[tmpC_s[tb]], writes=[s_x1[i]])
                if sg == NSEG - 1 and n == 3:
                    if i == 0:
                        dma(SP, DmaSem(K, "fws"), fw_b[:], final_w_d.partition_broadcast(128), writes=[s_fw, s_h2T])
                        ACT.begin(writes=[s_h2T])
                        DVE.begin(writes=[s_h2T])
                    final_tile(i)
    epoch()
    stop("C")

    SP.wait([(d.sem, d.n) for d in ys])


_CACHE = {}


def _rope_tables(pos):
    inv = (10000.0 ** (-np.arange(16, dtype=np.float32) / 16.0)).astype(np.float32)
    row = (pos // 64).astype(np.float32)
    col = (pos % 64).astype(np.float32)
    cosT = np.zeros((128, len(pos)), np.float32)
    sinT = np.zeros((128, len(pos)), np.float32)
    for p in range(128):
        i = p % 32
        ang = (row if i < 16 else col) * inv[i % 16]
        cosT[p] = np.cos(ang)
        sinT[p] = np.sin(ang) if (p % 64) < 32 else -np.sin(ang)
    return cosT, sinT


def _perm64():
    return np.concatenate([np.arange(0, 16), np.arange(32, 48), np.arange(16, 32), np.arange(48, 64)])


def kernel(x, c, ctx, c_ctx, w_ada, b_ada, norm1_w, w_in, q_norm_w, kv_norm_w, w_uq, w_ukv,
           lambda_q1, lambda_k1, lambda_q2, lambda_k2, subln_w, w_o, norm2_w, w_up,
           conv_w, conv_b, w_down, final_w):
    if "nc" not in _CACHE:
        _CACHE["nc"] = build_program().nc
    nc = _CACHE["nc"]
    in_maps = _prep(x, c, ctx, c_ctx, w_ada, b_ada, norm1_w, w_in, q_norm_w, kv_norm_w, w_uq, w_ukv,
                    lambda_q1, lambda_k1, lambda_q2, lambda_k2, subln_w, w_o, norm2_w, w_up,
                    conv_w, conv_b, w_down, final_w)
    res = run_bass_kernel_spmd(nc, in_maps, core_ids=list(range(8)))
    out = np.empty((4, SEQ, D), np.float32)
    for core in range(8):
        b, half = core // 2, core % 2
        out[b, half * 1024:(half + 1) * 1024] = res.results[core]["y"]
    return out


def _prep(x, c, ctx, c_ctx, w_ada, b_ada, norm1_w, w_in, q_norm_w, kv_norm_w, w_uq, w_ukv,
          lambda_q1, lambda_k1, lambda_q2, lambda_k2, subln_w, w_o, norm2_w, w_up,
          conv_w, conv_b, w_down, final_w):
    f = lambda a: np.ascontiguousarray(np.asarray(a, dtype=np.float32))
    x, c, ctx, c_ctx = f(x), f(c), f(ctx), f(c_ctx)
    p64 = _perm64()
    w_in0 = f(w_in)[0]
    cols = np.arange(3776)
    for base in (0, 1024):
        for hb in range(16):
            s0 = base + hb * 64
            cols[s0:s0 + 64] = s0 + p64
    cols[3712:3776] = 3712 + p64
    w_in_p = np.ascontiguousarray(w_in0[:, cols])
    w_uq0 = f(w_uq)[0]
    ucols = np.arange(1536)
    for h in range(8):
        s0 = h * 192 + 128
        ucols[s0:s0 + 64] = s0 + p64
    w_uq_p = np.ascontiguousarray(w_uq0[:, ucols])
    fm = lambda v, n: np.ascontiguousarray(f(v).reshape(n, 128).T)
    b_ada0 = f(b_ada)[0]
    common = {
        "ident": np.eye(128, dtype=np.float32),
        "w_ada": f(w_ada)[0], "b_adaT": fm(b_ada0, 96), "b_ada_row": b_ada0,
        "norm1T": fm(f(norm1_w)[0], 16), "norm2T": fm(f(norm2_w)[0], 16), "final_w": f(final_w),
        "w_in": w_in_p, "w_uq": w_uq_p, "w_ukv": f(w_ukv)[0],
        "qnwT": fm(f(q_norm_w)[0], 3), "kvnwT": fm(f(kv_norm_w)[0], 2),
        "lamb": np.concatenate([f(lambda_q1)[0], f(lambda_k1)[0], f(lambda_q2)[0], f(lambda_k2)[0]]),
        "sublnT": f(subln_w)[0].reshape(128, 1).copy(),
        "w_o": f(w_o)[0], "w_up": f(w_up)[0], "w_down": f(w_down)[0],
        "convT": np.ascontiguousarray(np.concatenate([f(conv_w)[0], f(conv_b)], 0).reshape(4, FC, 128).transpose(2, 1, 0)),
    }
    in_maps = []
    for core in range(8):
        b, half = core // 2, core % 2
        lo = half * 1024
        olo = (1 - half) * 1024
        xall = np.zeros((NH, D), np.float32)
        xall[0:1024] = x[b, lo:lo + 1024]
        pos_q = np.zeros(NQ, np.int64)
        pos_q[0:1024] = np.arange(lo, lo + 1024)
        masks = np.zeros((128, 2), np.float32)
        if lo - 1 >= 0:
            xall[1024] = x[b, lo - 1]
            pos_q[1024] = lo - 1
            masks[:, 0] = 1.0
        if lo + 1024 < SEQ:
            xall[1025] = x[b, lo + 1024]
            pos_q[1025] = lo + 1024
            masks[:, 1] = 1.0
        xall[1026:2050] = x[b, olo:olo + 1024]
        xall[2050:2306] = ctx[b]
        pos = np.concatenate([pos_q, np.arange(olo, olo + 1024)])
        cosT, sinT = _rope_tables(pos)
        cvec = np.stack([c[b].reshape(16, 128).T, c_ctx.reshape(16, 128).T], axis=-1)
        m = dict(common)
        m.update({"xall": xall, "cvec": np.ascontiguousarray(cvec, dtype=np.float32), "cosT": cosT, "sinT": sinT,
                  "masks": masks})
        in_maps.append(m)
    return in_maps
```
